# Optimizing a Trainium2 kernel written in Bass

```python
import math
import jax, jax.numpy as jnp
from jax import lax
import numpy as np

D_MODEL = 1024
BATCH = 8
SEQ = 2048
DEPTH = 2
DEC_BATCH = 16
DEC_SEQ = 4096
PAST_LEN = 128

CHUNK = 128
A_WIDTH = 1536
A_GROUPS = 12
A_GROUP_CH = A_WIDTH // A_GROUPS
B_WIDTH = 1024
HY_ORDER = 2
POS_EMB = 33
FILTER_HIDDEN = 64
DECAY_TARGET = 1e-2
FAST_DECAY_PCT = 0.3
SLOW_DECAY_PCT = 1.5
FILTER_EPS = 1e-6
IN_COLS = 2 * A_WIDTH + 3 * B_WIDTH + 2 * D_MODEL
P_HEADS = 8
P_DK = 256
N_KEYS = 128
N_EXPERTS = N_KEYS * N_KEYS
P_TOPK = 16
P_BLOCK = 128
RMS_EPS = 1e-6
LN_EPS = 1e-5

kernel_name = "hybrid_gmlp_hyena_peer_encoder"


def rms_norm(x, g):
    xf = x.astype(jnp.float32)
    y = xf * lax.rsqrt(jnp.mean(xf * xf, axis=-1, keepdims=True) + RMS_EPS)
    return (y * g.astype(jnp.float32)).astype(x.dtype)


def layer_norm(x, g, b):
    xf = x.astype(jnp.float32)
    mu = jnp.mean(xf, axis=-1, keepdims=True)
    xc = xf - mu
    y = xc * lax.rsqrt(jnp.mean(xc * xc, axis=-1, keepdims=True) + LN_EPS)
    return (y * g.astype(jnp.float32) + b.astype(jnp.float32)).astype(x.dtype)


def gmlp_branch(z, ln_g, ln_b, ws, bs):
    B, L, _ = z.shape
    a = jax.nn.gelu(z, approximate=False)
    u, v = jnp.split(a, 2, axis=-1)
    v = layer_norm(v, ln_g, ln_b)
    v = v.reshape(B, L // CHUNK, CHUNK, A_GROUPS, A_GROUP_CH)
    mixed = jnp.einsum('bcpgd,gqp->bcqgd', v, ws) + bs.T[None, None, :, :, None]
    return u * mixed.reshape(B, L, A_WIDTH)


def short_conv(x, w, b):
    xp = jnp.pad(x, ((0, 0), (1, 1), (0, 0)))
    return xp[:, :-2] * w[0] + xp[:, 1:-1] * w[1] + xp[:, 2:] * w[2] + b


def hyena_filters(L, w1, b1, fr1, w2, b2, fr2, w3, b3, fr3, w4):
    f32 = jnp.float32
    t = jnp.linspace(0.0, 1.0, L, dtype=f32)[:, None]
    bands = (POS_EMB - 1) // 2
    wpos = 2.0 * math.pi * jnp.arange(L, dtype=f32)[:, None] / L
    freqs = jnp.linspace(1e-4, bands - 1, bands, dtype=f32)[None, :]
    feat = jnp.concatenate([t, jnp.cos(freqs * wpos), -jnp.sin(freqs * wpos)], axis=-1)
    h = jnp.sin(fr1.astype(f32) * (feat @ w1.astype(f32) + b1.astype(f32)))
    h = jnp.sin(fr2.astype(f32) * (h @ w2.astype(f32) + b2.astype(f32)))
    h = jnp.sin(fr3.astype(f32) * (h @ w3.astype(f32) + b3.astype(f32)))
    h = (h @ w4.astype(f32)).reshape(L, HY_ORDER, 2, B_WIDTH)
    max_decay = math.log(DECAY_TARGET) / FAST_DECAY_PCT
    min_decay = math.log(DECAY_TARGET) / SLOW_DECAY_PCT
    deltas = jnp.linspace(min_decay, max_decay, B_WIDTH, dtype=f32)
    h = h * jnp.exp(-t[:, :, None, None] * jnp.abs(deltas))
    h_fwd = h[:, :, 0]
    h_bwd = h[1:, :, 1]
    h_fwd = h_fwd / (jnp.sum(jnp.abs(h_fwd), axis=0, keepdims=True) + FILTER_EPS)
    h_bwd = h_bwd / (jnp.sum(jnp.abs(h_bwd), axis=0, keepdims=True) + FILTER_EPS)
    k = jnp.concatenate([h_fwd, jnp.zeros((1, HY_ORDER, B_WIDTH), f32), h_bwd[::-1]], axis=0)
    return jnp.fft.rfft(k, axis=0)


def long_conv(z, k_f):
    L = z.shape[1]
    zf = jnp.fft.rfft(z, n=2 * L, axis=1)
    return jnp.fft.irfft(zf * k_f[None], n=2 * L, axis=1)[:, :L]


def hyena_branch(z, conv_w, conv_b, w1, b1, fr1, w2, b2, fr2, w3, b3, fr3, w4, skip):
    L = z.shape[1]
    k_f = hyena_filters(L, w1, b1, fr1, w2, b2, fr2, w3, b3, fr3, w4)
    zc = short_conv(z, conv_w, conv_b).astype(jnp.float32)
    v, x1, x2 = jnp.split(zc, 3, axis=-1)
    skip = skip.astype(jnp.float32)
    s = x1 * (long_conv(v, k_f[:, 0]) + v * skip[0])
    y = x2 * (long_conv(s, k_f[:, 1]) + s * skip[1])
    return y.astype(z.dtype)


def peer(x, wq, keys, u_tab, v_tab):
    B, L, D = x.shape
    xb = x.reshape(-1, P_BLOCK, D)

    def block(xt):
        q = (xt @ wq).reshape(P_BLOCK, P_HEADS, 2, P_DK // 2)
        s = jnp.einsum('thpd,hpnd->thpn', q, keys).astype(jnp.float32)
        top_s, top_i = lax.top_k(s, P_TOPK)
        cand_s = (top_s[:, :, 0, :, None] + top_s[:, :, 1, None, :]).reshape(P_BLOCK, P_HEADS, P_TOPK * P_TOPK)
        cand_i = (top_i[:, :, 0, :, None] * N_KEYS + top_i[:, :, 1, None, :]).reshape(P_BLOCK, P_HEADS, P_TOPK * P_TOPK)
        best_s, best_pos = lax.top_k(cand_s, P_TOPK)
        idx = jnp.take_along_axis(cand_i, best_pos, axis=-1)
        gate = jax.nn.softmax(best_s, axis=-1)
        h = jnp.einsum('td,thkd->thk', xt, u_tab[idx]).astype(jnp.float32)
        a = (gate * jax.nn.gelu(h, approximate=False)).astype(xt.dtype)
        return jnp.einsum('thk,thkd->td', a, v_tab[idx])

    return lax.map(block, xb).reshape(B, L, D)


def _trunk(x, norm_mix, w_in, gm_ln_g, gm_ln_b, gm_ws, gm_bs, hy_conv_w, hy_conv_b,
           hy_f_w1, hy_f_b1, hy_f_freq1, hy_f_w2, hy_f_b2, hy_f_freq2, hy_f_w3, hy_f_b3, hy_f_freq3,
           hy_f_w4, hy_skip, w_a_out, w_b_out, w_o, norm_ffn, peer_wq, peer_keys, peer_u, peer_v, final_norm):
    for l in range(DEPTH):
        h = rms_norm(x, norm_mix[l])
        proj = h @ w_in[l]
        z_a, z_b, z_g = jnp.split(proj, [2 * A_WIDTH, 2 * A_WIDTH + 3 * B_WIDTH], axis=-1)
        y_a = gmlp_branch(z_a, gm_ln_g[l], gm_ln_b[l], gm_ws[l], gm_bs[l]) @ w_a_out[l]
        y_b = hyena_branch(z_b, hy_conv_w[l], hy_conv_b[l],
                           hy_f_w1[l], hy_f_b1[l], hy_f_freq1[l],
                           hy_f_w2[l], hy_f_b2[l], hy_f_freq2[l],
                           hy_f_w3[l], hy_f_b3[l], hy_f_freq3[l],
                           hy_f_w4[l], hy_skip[l]) @ w_b_out[l]
        g_a, g_b = jnp.split(jax.nn.sigmoid(z_g), 2, axis=-1)
        x = x + (g_a * y_a + g_b * y_b) @ w_o[l]
        x = x + peer(rms_norm(x, norm_ffn[l]), peer_wq[l], peer_keys[l], peer_u[l], peer_v[l])
    return rms_norm(x, final_norm)


def setup_inputs(seed: int = 0) -> dict:
    key = jax.random.key(seed)
    ks = jax.random.split(key, 32)
    f32 = jnp.float32

    def nrm(k, shape, scale):
        return jax.random.normal(k, shape, f32) * scale

    def gain(k, shape):
        return 1.0 + 0.02 * jax.random.normal(k, shape, f32)

    return {
        "x_prompt": nrm(ks[0], (BATCH, SEQ, D_MODEL), 1.0),
        "x_sample": nrm(ks[1], (DEC_BATCH, DEC_SEQ, D_MODEL), 1.0),
        "norm_mix": gain(ks[2], (DEPTH, D_MODEL)),
        "w_in": nrm(ks[3], (DEPTH, D_MODEL, IN_COLS), D_MODEL ** -0.5),
        "gm_ln_g": gain(ks[4], (DEPTH, A_WIDTH)),
        "gm_ln_b": nrm(ks[5], (DEPTH, A_WIDTH), 0.02),
        "gm_ws": nrm(ks[6], (DEPTH, A_GROUPS, CHUNK, CHUNK), CHUNK ** -0.5),
        "gm_bs": gain(ks[7], (DEPTH, A_GROUPS, CHUNK)),
        "hy_conv_w": nrm(ks[8], (DEPTH, 3, 3 * B_WIDTH), 3 ** -0.5),
        "hy_conv_b": nrm(ks[9], (DEPTH, 3 * B_WIDTH), 0.02),
        "hy_f_w1": nrm(ks[10], (DEPTH, POS_EMB, FILTER_HIDDEN), POS_EMB ** -0.5),
        "hy_f_b1": nrm(ks[11], (DEPTH, FILTER_HIDDEN), 0.02),
        "hy_f_freq1": gain(ks[12], (DEPTH, FILTER_HIDDEN)),
        "hy_f_w2": nrm(ks[13], (DEPTH, FILTER_HIDDEN, FILTER_HIDDEN), FILTER_HIDDEN ** -0.5),
        "hy_f_b2": nrm(ks[14], (DEPTH, FILTER_HIDDEN), 0.02),
        "hy_f_freq2": gain(ks[15], (DEPTH, FILTER_HIDDEN)),
        "hy_f_w3": nrm(ks[16], (DEPTH, FILTER_HIDDEN, FILTER_HIDDEN), FILTER_HIDDEN ** -0.5),
        "hy_f_b3": nrm(ks[17], (DEPTH, FILTER_HIDDEN), 0.02),
        "hy_f_freq3": gain(ks[18], (DEPTH, FILTER_HIDDEN)),
        "hy_f_w4": nrm(ks[19], (DEPTH, FILTER_HIDDEN, HY_ORDER * 2 * B_WIDTH), FILTER_HIDDEN ** -0.5),
        "hy_skip": nrm(ks[20], (DEPTH, HY_ORDER, B_WIDTH), 1.0),
        "w_a_out": nrm(ks[21], (DEPTH, A_WIDTH, D_MODEL), A_WIDTH ** -0.5),
        "w_b_out": nrm(ks[22], (DEPTH, B_WIDTH, D_MODEL), B_WIDTH ** -0.5),
        "w_o": nrm(ks[23], (DEPTH, D_MODEL, D_MODEL), D_MODEL ** -0.5),
        "norm_ffn": gain(ks[24], (DEPTH, D_MODEL)),
        "peer_wq": nrm(ks[25], (DEPTH, D_MODEL, P_HEADS * P_DK), D_MODEL ** -0.5),
        "peer_keys": nrm(ks[26], (DEPTH, P_HEADS, 2, N_KEYS, P_DK // 2), (P_DK // 2) ** -0.5),
        "peer_u": nrm(ks[27], (DEPTH, N_EXPERTS, D_MODEL), D_MODEL ** -0.5),
        "peer_v": nrm(ks[28], (DEPTH, N_EXPERTS, D_MODEL), D_MODEL ** -0.5),
        "final_norm": gain(ks[29], (D_MODEL,)),
    }


def reference(x_prompt, x_sample, norm_mix, w_in, gm_ln_g, gm_ln_b, gm_ws, gm_bs, hy_conv_w, hy_conv_b,
              hy_f_w1, hy_f_b1, hy_f_freq1, hy_f_w2, hy_f_b2, hy_f_freq2, hy_f_w3, hy_f_b3, hy_f_freq3,
              hy_f_w4, hy_skip, w_a_out, w_b_out, w_o, norm_ffn, peer_wq, peer_keys, peer_u, peer_v, final_norm):
    weights = (norm_mix, w_in, gm_ln_g, gm_ln_b, gm_ws, gm_bs, hy_conv_w, hy_conv_b,
               hy_f_w1, hy_f_b1, hy_f_freq1, hy_f_w2, hy_f_b2, hy_f_freq2, hy_f_w3, hy_f_b3, hy_f_freq3,
               hy_f_w4, hy_skip, w_a_out, w_b_out, w_o, norm_ffn, peer_wq, peer_keys, peer_u, peer_v, final_norm)
    y_prompt = _trunk(x_prompt, *weights)
    y_sample = _trunk(x_sample, *weights)
    return (y_prompt, y_sample)
```

```python
import math
from contextlib import ExitStack

import numpy as np
import ml_dtypes

import concourse.bass as bass
import concourse.mybir as mybir
from concourse.bass_utils import run_bass_kernel_spmd

F32 = mybir.dt.float32
BF16 = mybir.dt.bfloat16
U32 = mybir.dt.uint32
AF = mybir.ActivationFunctionType
ALU = mybir.AluOpType
AX = mybir.AxisListType

D = 1024
DEPTH = 2
A_W = 1536
B_W = 1024
IN_COLS = 8192
NEXP = 16384
RMS_EPS = 1e-6
LN_EPS = 1e-5
FILTER_EPS = 1e-6
PI = math.pi

ENG_BLK = {"sp": "sync", "pe": "tensor", "act": "scalar", "dve": "vector", "pool": "gpsimd"}


class Ctx:
    def __init__(self, nc, stack, n_dma_sems=48):
        self.nc = nc
        self.semh = {}
        self.semval = {}
        for e in ENG_BLK:
            if e == "sp":
                continue
            self.semh[e] = stack.enter_context(nc.semaphore("sem_" + e))
            self.semval[e] = 0
        self.free_dma = []
        for i in range(n_dma_sems):
            nm = "dq%d" % i
            self.semh[nm] = stack.enter_context(nc.semaphore(nm))
            self.semval[nm] = 0
            self.free_dma.append(nm)
        self.dma_key2sem = {}
        self.known = {e: {} for e in ENG_BLK}
        self.reset_phase()

    def reset_phase(self):
        self.ops = {e: [] for e in ENG_BLK}
        self.lastw = {}
        self.readers = {}
        for k, s in self.dma_key2sem.items():
            self.free_dma.append(s)
        self.dma_key2sem = {}

    def _deps(self, e, r, w):
        deps = {}

        def add(m):
            sk, v = m
            if deps.get(sk, 0) < v:
                deps[sk] = v

        for k in r:
            if k in self.lastw:
                add(self.lastw[k])
        for k in w:
            if k in self.lastw:
                add(self.lastw[k])
            for m in self.readers.get(k, ()):
                add(m)
        waits = []
        for sk, v in deps.items():
            if e == "pe" and sk == "pe":
                continue
            if self.known[e].get(sk, 0) >= v:
                continue
            waits.append((sk, v))
            self.known[e][sk] = v
        return waits

    def _commit(self, mark, r, w):
        for k in w:
            self.lastw[k] = mark
            self.readers[k] = []
        for k in r:
            if k not in w:
                self.readers.setdefault(k, []).append(mark)

    def defer_begin(self):
        self._defer = []

    def defer_end(self):
        d, self._defer = self._defer, None
        return d

    def replay(self, pending, n):
        while n > 0 and pending:
            kind, args = pending.pop(0)
            if kind == "op":
                self.op(*args)
            else:
                self.dma(*args)
            n -= 1

    def op(self, e, fn, r=(), w=()):
        if getattr(self, "_defer", None) is not None:
            self._defer.append(("op", (e, fn, tuple(r), tuple(w))))
            return
        r = tuple(r)
        w = tuple(w)
        waits = self._deps(e, r, w)
        self.semval[e] += 1
        mark = (e, self.semval[e])
        self.ops[e].append((waits, [fn], e, 1))
        self._commit(mark, r, w)

    def dma(self, e, fns, key, r=(), w=()):
        if getattr(self, "_defer", None) is not None:
            self._defer.append(("dma", (e, fns, key, tuple(r), tuple(w))))
            return
        if not isinstance(fns, (list, tuple)):
            fns = [fns]
        r = tuple(r)
        w = tuple(w)
        waits = self._deps(e, r, w)
        if key not in self.dma_key2sem:
            self.dma_key2sem[key] = self.free_dma.pop()
        sk = self.dma_key2sem[key]
        self.semval[sk] += 16 * len(fns)
        mark = (sk, self.semval[sk])
        self.ops[e].append((waits, list(fns), sk, 16))
        self._commit(mark, r, w)

    def flush(self):
        nc = self.nc
        finals = dict(self.semval)
        with nc.Block() as blk:
            for e, bname in ENG_BLK.items():
                ops = self.ops[e]
                known = self.known[e]

                def body(eng, ops=ops, known=known):
                    for waits, fns, sk, inc in ops:
                        for wk, v in waits:
                            eng.wait_ge(self.semh[wk], v)
                        for fn in fns:
                            fn(eng).then_inc(self.semh[sk], inc)
                    for sk, v in finals.items():
                        if v > 0 and known.get(sk, 0) < v:
                            eng.wait_ge(self.semh[sk], v)
                            known[sk] = v

                getattr(blk, bname)(body)
        self.reset_phase()


def dft_mats(L):
    t = np.arange(L, dtype=np.float64)[:, None]
    f = np.arange(L, dtype=np.float64)[None, :]
    ang = 2.0 * np.pi * (f + 0.5) * t / (2 * L)
    c = np.cos(ang)
    s = -np.sin(ang)
    nb = L // 128
    fwd = np.empty((L, 2 * L), np.float64)
    fv = fwd.reshape(L, nb, 2, 128)
    fv[:, :, 0, :] = c.reshape(L, nb, 128)
    fv[:, :, 1, :] = s.reshape(L, nb, 128)
    inv = fwd.T / L
    fwdP = fwd.reshape(nb, 128, nb, 256).transpose(2, 1, 0, 3)
    invP = inv.reshape(2 * nb, 128, nb, 128).transpose(2, 1, 0, 3)
    return (np.ascontiguousarray(fwdP).astype(ml_dtypes.bfloat16),
            np.ascontiguousarray(invP).astype(ml_dtypes.bfloat16))


def filter_feats(L):
    f32 = np.float32
    t = np.linspace(0.0, 1.0, L, dtype=f32)[:, None]
    bands = 16
    wpos = (2.0 * math.pi * np.arange(L, dtype=f32)[:, None] / L).astype(f32)
    freqs = np.linspace(1e-4, bands - 1, bands, dtype=f32)[None, :]
    feat = np.concatenate([t, np.cos(freqs * wpos), -np.sin(freqs * wpos)], axis=-1).astype(f32)
    featT = np.ascontiguousarray(feat.T)
    negt = np.ascontiguousarray((-t[:, 0]).reshape(L // 128, 128).T).astype(f32)
    return featT, negt


def abs_deltas():
    max_decay = math.log(1e-2) / 0.3
    min_decay = math.log(1e-2) / 1.5
    d = np.linspace(min_decay, max_decay, B_W, dtype=np.float32)
    return np.abs(d).reshape(1, B_W).astype(np.float32)


WEIGHT_SPECS = [
    ("norm_mix", [DEPTH, D]), ("w_in", [DEPTH, D, IN_COLS]), ("gm_ln_g", [DEPTH, A_W]), ("gm_ln_b", [DEPTH, A_W]),
    ("gm_ws", [DEPTH, 12, 128, 128]), ("gm_bs", [DEPTH, 12, 128]), ("hy_conv_w", [DEPTH, 3, 3 * B_W]),
    ("hy_conv_b", [DEPTH, 3 * B_W]), ("hy_f_w1", [DEPTH, 33, 64]), ("hy_f_b1", [DEPTH, 64]),
    ("hy_f_freq1", [DEPTH, 64]), ("hy_f_w2", [DEPTH, 64, 64]), ("hy_f_b2", [DEPTH, 64]), ("hy_f_freq2", [DEPTH, 64]),
    ("hy_f_w3", [DEPTH, 64, 64]), ("hy_f_b3", [DEPTH, 64]), ("hy_f_freq3", [DEPTH, 64]),
    ("hy_f_w4", [DEPTH, 64, 4 * B_W]), ("hy_skip", [DEPTH, 2, B_W]), ("w_a_out", [DEPTH, A_W, D]),
    ("w_b_out", [DEPTH, B_W, D]), ("w_o", [DEPTH, D, D]), ("norm_ffn", [DEPTH, D]), ("peer_wq", [DEPTH, D, 2048]),
    ("peer_keys", [DEPTH, 8, 2, 128, 128]), ("peer_u", [DEPTH, NEXP, D]), ("peer_v", [DEPTH, NEXP, D]),
    ("final_norm", [D]),
]


class Prog:
    def __init__(self, seqs, dbg=None, phases=None):
        self.seqs = list(seqs)
        self.T = sum(seqs)
        self.NT = self.T // 128
        self.Ls = sorted(set(seqs))
        self.dbg = dbg or ()
        self.phases = phases
        self.nc = bass.Bass("TRN2", target_bir_lowering=False)
        nc = self.nc
        self.inp = {}
        self.inp["x"] = nc.dram_tensor("x", [self.T, D], F32, kind="ExternalInput").ap()
        for nm, shp in WEIGHT_SPECS:
            self.inp[nm] = nc.dram_tensor(nm, shp, F32, kind="ExternalInput").ap()
        self.inp["ident_b"] = nc.dram_tensor("ident_b", [128, 128], BF16, kind="ExternalInput").ap()
        self.inp["ident_f"] = nc.dram_tensor("ident_f", [128, 128], F32, kind="ExternalInput").ap()
        self.inp["absdelta"] = nc.dram_tensor("absdelta", [1, B_W], F32, kind="ExternalInput").ap()
        self.inp["iota16"] = nc.dram_tensor("iota16", [128, 16], F32, kind="ExternalInput").ap()
        for L in self.Ls:
            nb = L // 128
            self.inp["fwd%d" % L] = nc.dram_tensor("fwd%d" % L, [nb, 128, nb, 256], BF16, kind="ExternalInput").ap()
            self.inp["inv%d" % L] = nc.dram_tensor("inv%d" % L, [nb, 128, 2 * nb, 128], BF16, kind="ExternalInput").ap()
            self.inp["featT%d" % L] = nc.dram_tensor("featT%d" % L, [33, L], F32, kind="ExternalInput").ap()
            self.inp["negt%d" % L] = nc.dram_tensor("negt%d" % L, [128, L // 128], F32, kind="ExternalInput").ap()
        self.y = nc.dram_tensor("y", [self.T, D], F32, kind="ExternalOutput").ap()

        def scratch(name, shape, dt):
            kind = "ExternalOutput" if name in self.dbg else "Internal"
            return nc.dram_tensor(name, shape, dt, kind=kind).ap()

        T = self.T
        self.s_win = scratch("s_win", [DEPTH, D, IN_COLS], BF16)
        self.s_wao = scratch("s_wao", [DEPTH, A_W, D], BF16)
        self.s_wbo = scratch("s_wbo", [DEPTH, B_W, D], BF16)
        self.s_wo = scratch("s_wo", [DEPTH, D, D], BF16)
        self.s_wq = scratch("s_wq", [DEPTH, D, 2048], BF16)
        self.s_uv = [scratch("s_uv%d" % l, [NEXP, 2, D], BF16) for l in range(DEPTH)]
        self.s_xres = scratch("s_xres", [T, D], F32)
        self.s_zb = scratch("s_zb", [T, 3 * B_W], BF16)
        self.s_xc = scratch("s_xc", [T, 2, B_W], BF16)
        self.s_gaya = scratch("s_gaya", [T, D], BF16)
        self.s_gb = scratch("s_gb", [T, D], BF16)
        self.s_yb = scratch("s_yb", [T, D], BF16)
        self.s_kf = {L: scratch("s_kf%d" % L, [2 * L, 2, B_W], BF16) for L in self.Ls}

    def build(self):
        nc = self.nc
        with ExitStack() as stack:
            self.cx = Ctx(nc, stack)
            self.ps = stack.enter_context(nc.psum_tensor("ps", [128, 8, 512], F32))
            self.identb = stack.enter_context(nc.sbuf_tensor("identb", [128, 128], BF16))
            self.identf = stack.enter_context(nc.sbuf_tensor("identf", [128, 128], F32))
            cx = self.cx
            cx.dma("sp", lambda e: e.dma_start(out=self.identb[:], in_=self.inp["ident_b"][:, :]), "c0", w=["identb"])
            cx.dma("sp", lambda e: e.dma_start(out=self.identf[:], in_=self.inp["ident_f"][:, :]), "c1", w=["identf"])
            cx.flush()
            ph = self.phases
            if ph is None or "W" in ph:
                self.phase_W()
            for l in range(DEPTH):
                if ph is None or "A1" in ph:
                    self.phase_A1(l)
                if ph is None or "A2" in ph:
                    self.phase_A2(l)
                if ph is None or "F" in ph:
                    for L in self.Ls:
                        self.phase_F(l, L)
                if ph is None or "B" in ph:
                    self.phase_B(l)
                if ph is None or "C" in ph or "C1" in ph:
                    self.phase_C1(l)
                if ph is None or "C" in ph or "C2" in ph:
                    self.phase_C2(l)
                if ph is not None and "L0" in ph:
                    break
        return nc

    def uname(self, name):
        self._uid = getattr(self, "_uid", 0) + 1
        return "%s_%d" % (name, self._uid)

    def psb(self, b):
        return self.ps[:, b, :].bitcast(BF16)

    def phase_W(self):
        nc, cx = self.nc, self.cx
        jobs = [(self.inp["w_in"], self.s_win), (self.inp["w_a_out"], self.s_wao), (self.inp["w_b_out"], self.s_wbo),
                (self.inp["w_o"], self.s_wo), (self.inp["peer_wq"], self.s_wq)]
        CH = 4096
        NS = 4
        with ExitStack() as st:
            fb = [st.enter_context(nc.sbuf_tensor(self.uname("wf%d" % i), [128, CH], F32)) for i in range(NS)]
            bb = [st.enter_context(nc.sbuf_tensor(self.uname("wb%d" % i), [128, CH], BF16)) for i in range(NS)]
            it = 0
            engs = ["dve", "pool"]
            work = []
            for src, dst in jobs:
                n = int(np.prod(src.shape)) // 128
                srcf = _flat128(src)
                dstf = _flat128(dst)
                for c0 in range(0, n, CH):
                    cw = min(CH, n - c0)
                    work.append((srcf[:, c0:c0 + cw], dstf[:, c0:c0 + cw], cw, None))
            for l in range(DEPTH):
                for which, nm in enumerate(("peer_u", "peer_v")):
                    srcv = self.inp[nm][l].rearrange("(p r) d -> p r d", p=128)
                    dstv = self.s_uv[l].rearrange("(p r) two d -> p r two d", p=128)[:, :, which, :]
                    for r0 in range(0, 128, 4):
                        work.append((srcv[:, r0:r0 + 4, :], dstv[:, r0:r0 + 4, :], CH, 4))
            for srca, dsta, cw, rows in work:
                s_ = it % NS
                fview = fb[s_][:, 0:cw] if rows is None else fb[s_][:, 0:cw].rearrange("p (r d) -> p r d", r=rows)
                bview = bb[s_][:, 0:cw] if rows is None else bb[s_][:, 0:cw].rearrange("p (r d) -> p r d", r=rows)
                cx.dma("sp", lambda e, fview=fview, srca=srca: e.dma_start(out=fview, in_=srca), ("wf", s_), w=[("wf", s_)])
                eng = engs[it % 2]
                cx.op(eng, lambda e, s_=s_, cw=cw: e.tensor_copy(out=bb[s_][:, 0:cw], in_=fb[s_][:, 0:cw]), r=[("wf", s_)], w=[("wb", s_)])
                cx.dma("act", lambda e, bview=bview, dsta=dsta: e.dma_start(out=dsta, in_=bview), ("wb", s_), r=[("wb", s_)])
                it += 1
            cx.flush()

    def rms_to_hT(self, xt, xkey, gt, tl, bank, hn_out_key="hn", hT_key="hT"):
        cx = self.cx
        junk, ss, rstd, hn, hT = tl["junk"], tl["ss"], tl["rstd"], tl["hn"], tl["hT"]
        cx.op("act", lambda e: e.activation(out=junk[:], in_=xt[:], func=AF.Square, accum_out=ss[:, 0:1]),
              r=[xkey], w=["junk", "ss"])
        cx.op("dve", lambda e: e.tensor_scalar(out=rstd[:], in0=ss[:], scalar1=1.0 / D, scalar2=RMS_EPS,
                                               op0=ALU.mult, op1=ALU.add), r=["ss"], w=["rstd"])
        cx.op("act", lambda e: e.sqrt(out=rstd[:], in_=rstd[:]), r=["rstd"], w=["rstd"])
        cx.op("dve", lambda e: e.reciprocal(out=rstd[:], in_=rstd[:]), r=["rstd"], w=["rstd"])
        cx.op("dve", lambda e: e.scalar_tensor_tensor(out=hn[:], in0=xt[:], scalar=rstd[:, 0:1], in1=gt[:],
                                                      op0=ALU.mult, op1=ALU.mult), r=[xkey, "rstd", "g"], w=[hn_out_key])
        pv = self.psb(bank)
        for k in range(8):
            cx.op("pe", lambda e, k=k: e.transpose(out=pv[:, k * 128:(k + 1) * 128], in_=hn[:, k * 128:(k + 1) * 128],
                                                   identity=self.identb[:]), r=[hn_out_key, "identb"], w=[("ps", bank)])
        cx.op("act", lambda e: e.copy(out=hT[:].rearrange("p k t -> p (k t)"), in_=pv[:, :]), r=[("ps", bank)], w=[hT_key])

    def phase_A1(self, l):
        nc, cx, ps = self.nc, self.cx, self.ps
        with ExitStack() as st:
            def sb(name, shape, dt):
                return st.enter_context(nc.sbuf_tensor(self.uname(name), shape, dt))
            win = sb("a1_win", [128, 8, 4096], BF16)
            wao = sb("a1_wao", [128, 12, 1024], BF16)
            wsn = sb("a1_wsn", [128, 12, 128], F32)
            wsb = sb("a1_wsb", [128, 12, 128], BF16)
            wsT = sb("a1_wsT", [128, 12, 128], BF16)
            bsT = sb("a1_bsT", [128, 12], F32)
            gmix = sb("a1_g", [128, D], F32)
            lng = sb("a1_lng", [128, A_W], F32)
            lnb = sb("a1_lnb", [128, A_W], F32)
            xts = [sb("a1_x%d" % i, [128, D], F32) for i in range(2)]
            tl = dict(junk=sb("a1_junk", [128, D], BF16), ss=sb("a1_ss", [128, 1], F32), rstd=sb("a1_rstd", [128, 1], F32),
                      hn=sb("a1_hn", [128, D], BF16), hT=sb("a1_hT", [128, 8, 128], BF16))
            a_s = [sb("a1_a%d" % i, [128, 2 * A_W], F32) for i in range(2)]
            ga_s = [sb("a1_ga%d" % i, [128, D], BF16) for i in range(2)]
            hns = [sb("a1_hn%d" % i, [128, D], BF16) for i in range(2)]
            hTs = [sb("a1_hT%d" % i, [128, 8, 128], BF16) for i in range(2)]
            stats = sb("a1_stats", [128, 3, 6], F32)
            mv = sb("a1_mv", [128, 2], F32)
            lrs = sb("a1_lrs", [128, 1], F32)
            vn32 = sb("a1_vn32", [128, A_W], F32)
            vn = sb("a1_vn", [128, A_W], BF16)
            gA = sb("a1_gA", [128, A_W], BF16)
            gAT = sb("a1_gAT", [128, 12, 128], BF16)
            outs = [sb("a1_o%d" % i, [128, D], BF16) for i in range(2)]

            wl = self.s_win[l]
            cx.dma("sp", [lambda e, k=k: e.dma_start(out=win[:, k, 0:3072], in_=wl[k * 128:(k + 1) * 128, 0:3072]) for k in range(8)]
                   + [lambda e, k=k: e.dma_start(out=win[:, k, 3072:4096], in_=wl[k * 128:(k + 1) * 128, 6144:7168]) for k in range(8)],
                   "win", w=["win"])
            cx.dma("sp", lambda e: e.dma_start(out=wao[:], in_=self.s_wao[l].rearrange("(j p) n -> p j n", p=128)), "wao", w=["wao"])
            cx.dma("sp", lambda e: e.dma_start(out=wsn[:], in_=self.inp["gm_ws"][l].rearrange("g q p -> q g p")), "wsn", w=["wsn"])
            cx.dma("sp", lambda e: e.dma_start(out=bsT[:], in_=self.inp["gm_bs"][l].rearrange("g q -> q g"),
                                               allow_slow_non_contiguous=True), "bsT", w=["bsT"])
            cx.dma("sp", lambda e: e.dma_start(out=gmix[:], in_=self.inp["norm_mix"][l:l + 1, :].partition_broadcast(128)), "g", w=["g"])
            cx.dma("sp", lambda e: e.dma_start(out=lng[:], in_=self.inp["gm_ln_g"][l:l + 1, :].partition_broadcast(128)), "lng", w=["lng"])
            cx.dma("sp", lambda e: e.dma_start(out=lnb[:], in_=self.inp["gm_ln_b"][l:l + 1, :].partition_broadcast(128)), "lnb", w=["lnb"])
            cx.op("dve", lambda e: e.tensor_copy(out=wsb[:], in_=wsn[:]), r=["wsn"], w=["wsb"])
            for half in range(2):
                bank = 6 + half
                pv = self.psb(bank)
                gs = list(range(half * 8, min(12, half * 8 + 8)))
                for j, g in enumerate(gs):
                    cx.op("pe", lambda e, j=j, g=g, pv=pv: e.transpose(out=pv[:, j * 128:(j + 1) * 128], in_=wsb[:, g, :],
                                                                      identity=self.identb[:]), r=["wsb"], w=[("ps", bank)])
                n = len(gs)
                cx.op("act", lambda e, pv=pv, n=n, g0=gs[0]: e.copy(
                    out=wsT[:, g0:g0 + n, :].rearrange("p g q -> p (g q)"), in_=pv[:, 0:n * 128]), r=[("ps", bank)], w=["wsT"])

            src = self.inp["x"] if l == 0 else self.s_xres

            def S1(i):
                p = i % 2
                xt = xts[p]
                r0 = i * 128
                a, ga = a_s[p], ga_s[p]
                cx.dma("sp", lambda e: e.dma_start(out=xt[:], in_=src[r0:r0 + 128, :]), ("x", p), w=[("x", p)])
                tlp = dict(tl); tlp["hn"] = hns[p]; tlp["hT"] = hTs[p]
                self.rms_to_hT(xt, ("x", p), gmix, tlp, bank=2, hn_out_key=("hn", p), hT_key=("hT", p))
                hT = hTs[p]
                for nb in range(8):
                    bank = nb % 2
                    for k in range(8):
                        cx.op("pe", lambda e, nb=nb, k=k, bank=bank: e.matmul(
                            ps[:, bank, :], lhsT=hT[:, k, :], rhs=win[:, k, nb * 512:(nb + 1) * 512], start=(k == 0), stop=(k == 7)),
                            r=[("hT", p), "win"], w=[("ps", bank)])
                    if nb < 6:
                        cx.op("act", lambda e, nb=nb, bank=bank: e.activation(out=a[:, nb * 512:(nb + 1) * 512], in_=ps[:, bank, :], func=AF.Gelu),
                              r=[("ps", bank)], w=[("a", nb, p)])
                    else:
                        cx.op("act", lambda e, nb=nb, bank=bank: e.activation(out=ga[:, (nb - 6) * 512:(nb - 5) * 512], in_=ps[:, bank, :],
                                                                             func=AF.Sigmoid), r=[("ps", bank)], w=[("ga", p)])

            def S2(i, pend):
                p = i % 2
                r0 = i * 128
                a, ga = a_s[p], ga_s[p]
                n1 = 4 * 9
                cx.replay(pend, 15)
                for c in range(3):
                    cx.op("dve", lambda e, c=c: e.bn_stats(out=stats[:, c, :], in_=a[:, A_W + c * 512:A_W + (c + 1) * 512]),
                          r=[("a", 3 + c, p)], w=["stats"])
                cx.op("dve", lambda e: e.bn_aggr(out=mv[:], in_=stats[:].rearrange("p c s -> p (c s)")), r=["stats"], w=["mv"])
                cx.op("dve", lambda e: e.tensor_scalar_add(out=lrs[:], in0=mv[:, 1:2], scalar1=LN_EPS), r=["mv"], w=["lrs"])
                cx.op("act", lambda e: e.sqrt(out=lrs[:], in_=lrs[:]), r=["lrs"], w=["lrs"])
                cx.op("dve", lambda e: e.reciprocal(out=lrs[:], in_=lrs[:]), r=["lrs"], w=["lrs"])
                cx.op("dve", lambda e: e.tensor_scalar(out=vn32[:], in0=a[:, A_W:2 * A_W], scalar1=mv[:, 0:1], scalar2=lrs[:, 0:1],
                                                       op0=ALU.subtract, op1=ALU.mult), r=[("a", 3, p), ("a", 4, p), ("a", 5, p), "mv", "lrs"], w=["vn32"])
                cx.op("pool", lambda e: e.tensor_tensor(out=vn32[:], in0=vn32[:], in1=lng[:], op=ALU.mult), r=["vn32", "lng"], w=["vn32"])
                cx.op("dve", lambda e: e.tensor_tensor(out=vn[:], in0=vn32[:], in1=lnb[:], op=ALU.add), r=["vn32", "lnb"], w=["vn"])
                cx.replay(pend, n1)
                for g in range(12):
                    bank = 3 + g // 4
                    cx.op("pe", lambda e, g=g, bank=bank: e.matmul(ps[:, bank, (g % 4) * 128:(g % 4 + 1) * 128], lhsT=wsT[:, g, :],
                                                                  rhs=vn[:, g * 128:(g + 1) * 128], start=True, stop=True),
                          r=["wsT", "vn"], w=[("ps", bank)])
                cx.replay(pend, 18)
                for g in range(12):
                    bank = 3 + g // 4
                    cx.op("dve", lambda e, g=g, bank=bank: e.scalar_tensor_tensor(
                        out=gA[:, g * 128:(g + 1) * 128], in0=ps[:, bank, (g % 4) * 128:(g % 4 + 1) * 128], scalar=bsT[:, g:g + 1],
                        in1=a[:, g * 128:(g + 1) * 128], op0=ALU.add, op1=ALU.mult),
                        r=[("ps", bank), "bsT", ("a", g // 4, p)], w=["gA"])
                for half in range(2):
                    bank = 6 + half
                    pv = self.psb(bank)
                    js = list(range(half * 8, min(12, half * 8 + 8)))
                    for jj, j in enumerate(js):
                        cx.op("pe", lambda e, jj=jj, j=j, pv=pv: e.transpose(out=pv[:, jj * 128:(jj + 1) * 128], in_=gA[:, j * 128:(j + 1) * 128],
                                                                            identity=self.identb[:]), r=["gA"], w=[("ps", bank)])
                    n = len(js)
                    cx.op("act", lambda e, pv=pv, n=n, j0=js[0]: e.copy(out=gAT[:, j0:j0 + n, :].rearrange("p j t -> p (j t)"),
                                                                       in_=pv[:, 0:n * 128]), r=[("ps", bank)], w=["gAT"])
                cx.replay(pend, len(pend))
                ot = outs[p]
                for nb in range(2):
                    bank = 3 + nb
                    for j in range(12):
                        cx.op("pe", lambda e, nb=nb, j=j, bank=bank: e.matmul(ps[:, bank, :], lhsT=gAT[:, j, :],
                                                                             rhs=wao[:, j, nb * 512:(nb + 1) * 512], start=(j == 0), stop=(j == 11)),
                              r=["gAT", "wao"], w=[("ps", bank)])
                    cx.op("dve", lambda e, nb=nb, bank=bank: e.tensor_tensor(out=ot[:, nb * 512:(nb + 1) * 512], in0=ps[:, bank, :],
                                                                            in1=ga[:, nb * 512:(nb + 1) * 512], op=ALU.mult),
                          r=[("ps", bank), ("ga", p)], w=[("o", p)])
                cx.dma("pool", lambda e: e.dma_start(out=self.s_gaya[r0:r0 + 128, :], in_=ot[:]), ("o", p), r=[("o", p)])

            S1(0)
            for i in range(self.NT):
                pend = []
                if i + 1 < self.NT:
                    cx.defer_begin()
                    S1(i + 1)
                    pend = cx.defer_end()
                S2(i, pend)
            cx.flush()

    def phase_A2(self, l):
        nc, cx, ps = self.nc, self.cx, self.ps
        with ExitStack() as st:
            def sb(name, shape, dt):
                return st.enter_context(nc.sbuf_tensor(self.uname(name), shape, dt))
            win = sb("a2_win", [128, 8, 4096], BF16)
            gmix = sb("a2_g", [128, D], F32)
            xts = [sb("a2_x%d" % i, [128, D], F32) for i in range(2)]
            tl = dict(junk=sb("a2_junk", [128, D], BF16), ss=sb("a2_ss", [128, 1], F32), rstd=sb("a2_rstd", [128, 1], F32),
                      hn=sb("a2_hn", [128, D], BF16), hT=sb("a2_hT", [128, 8, 128], BF16))
            hns = [sb("a2_hn%d" % i, [128, D], BF16) for i in range(2)]
            hTs = [sb("a2_hT%d" % i, [128, 8, 128], BF16) for i in range(2)]
            zbs = [sb("a2_zb%d" % i, [128, 3 * B_W], BF16) for i in range(2)]
            gbs = [sb("a2_gb%d" % i, [128, D], BF16) for i in range(2)]
            wl = self.s_win[l]
            cx.dma("sp", [lambda e, k=k: e.dma_start(out=win[:, k, 0:3072], in_=wl[k * 128:(k + 1) * 128, 3072:6144]) for k in range(8)]
                   + [lambda e, k=k: e.dma_start(out=win[:, k, 3072:4096], in_=wl[k * 128:(k + 1) * 128, 7168:8192]) for k in range(8)],
                   "win", w=["win"])
            cx.dma("sp", lambda e: e.dma_start(out=gmix[:], in_=self.inp["norm_mix"][l:l + 1, :].partition_broadcast(128)), "g", w=["g"])
            src = self.inp["x"] if l == 0 else self.s_xres

            def Sa(i):
                p = i % 2
                xt = xts[p]
                r0 = i * 128
                cx.dma("sp", lambda e: e.dma_start(out=xt[:], in_=src[r0:r0 + 128, :]), ("x", p), w=[("x", p)])
                tlp = dict(tl); tlp["hn"] = hns[p]; tlp["hT"] = hTs[p]
                self.rms_to_hT(xt, ("x", p), gmix, tlp, bank=2 + p, hn_out_key=("hn", p), hT_key=("hT", p))

            def Sb(i, pend):
                p = i % 2
                r0 = i * 128
                hT = hTs[p]
                zb, gb = zbs[p], gbs[p]
                for nb in range(8):
                    bank = nb % 2
                    for k in range(8):
                        cx.op("pe", lambda e, nb=nb, k=k, bank=bank: e.matmul(
                            ps[:, bank, :], lhsT=hT[:, k, :], rhs=win[:, k, nb * 512:(nb + 1) * 512], start=(k == 0), stop=(k == 7)),
                            r=[("hT", p), "win"], w=[("ps", bank)])
                    if nb < 6:
                        if nb % 2 == 0:
                            cx.op("dve", lambda e, nb=nb, bank=bank: e.tensor_copy(out=zb[:, nb * 512:(nb + 1) * 512], in_=ps[:, bank, :]),
                                  r=[("ps", bank)], w=[("zb", p)])
                        else:
                            cx.op("act", lambda e, nb=nb, bank=bank: e.copy(out=zb[:, nb * 512:(nb + 1) * 512], in_=ps[:, bank, :]),
                                  r=[("ps", bank)], w=[("zb", p)])
                    else:
                        cx.op("act", lambda e, nb=nb, bank=bank: e.activation(out=gb[:, (nb - 6) * 512:(nb - 5) * 512], in_=ps[:, bank, :],
                                                                             func=AF.Sigmoid), r=[("ps", bank)], w=[("gb", p)])
                    if nb == 1:
                        cx.replay(pend, len(pend))
                cx.dma("pool", lambda e: e.dma_start(out=self.s_zb[r0:r0 + 128, :], in_=zb[:]), ("zb", p), r=[("zb", p)])
                cx.dma("pool", lambda e: e.dma_start(out=self.s_gb[r0:r0 + 128, :], in_=gb[:]), ("gb", p), r=[("gb", p)])

            Sa(0)
            for i in range(self.NT):
                pend = []
                if i + 1 < self.NT:
                    cx.defer_begin()
                    Sa(i + 1)
                    pend = cx.defer_end()
                Sb(i, pend)
            cx.flush()

    def phase_F(self, l, L):
        nc, cx, ps = self.nc, self.cx, self.ps
        NTs = L // 128
        inp = self.inp
        with ExitStack() as st_outer:
          hA = st_outer.enter_context(nc.sbuf_tensor(self.uname("f_hA"), [64, L], F32))
          h3b = st_outer.enter_context(nc.sbuf_tensor(self.uname("f_h3b"), [64, L], BF16))
          with ExitStack() as st:
            def sb(name, shape, dt):
                return st.enter_context(nc.sbuf_tensor(self.uname(name), shape, dt))
            featT = sb("f_feat", [33, L], F32)
            w1 = sb("f_w1", [33, 64], F32)
            w2 = sb("f_w2", [64, 64], F32)
            w3 = sb("f_w3", [64, 64], F32)
            bfr = sb("f_bfr", [64, 6], F32)
            hB = sb("f_hB", [64, L], F32)
            arg = sb("f_arg", [64, 512], F32)
            msk = sb("f_msk", [64, 512], F32)
            cx.dma("sp", lambda e: e.dma_start(out=featT[:], in_=inp["featT%d" % L][:, :]), "feat", w=["feat"])
            cx.dma("sp", lambda e: e.dma_start(out=w1[:], in_=inp["hy_f_w1"][l]), "w1", w=["w1"])
            cx.dma("sp", lambda e: e.dma_start(out=w2[:], in_=inp["hy_f_w2"][l]), "w2", w=["w2"])
            cx.dma("sp", lambda e: e.dma_start(out=w3[:], in_=inp["hy_f_w3"][l]), "w3", w=["w3"])
            names = ["hy_f_b1", "hy_f_b2", "hy_f_b3", "hy_f_freq1", "hy_f_freq2", "hy_f_freq3"]
            cx.dma("sp", [lambda e, i=i, nm=nm: e.dma_start(out=bfr[:, i:i + 1], in_=inp[nm][l].rearrange("(h o) -> h o", o=1))
                          for i, nm in enumerate(names)], "bfr", w=["bfr"])

            chain = [(featT, 33, w1, hA, "feat", "w1", "hA"), (hA, 64, w2, hB, "hA", "w2", "hB"), (hB, 64, w3, h3b, "hB", "w3", "hA2")]
            nblk = (L + 511) // 512
            for li, (src, kk, wt, dst, sk_, wk_, dk_) in enumerate(chain):
                for b in range(nblk):
                    c0 = b * 512
                    cw = min(512, L - c0)
                    bank = b % 2
                    rk = sk_ if li != 2 else "hB"
                    cx.op("pe", lambda e, src=src, kk=kk, wt=wt, c0=c0, cw=cw, bank=bank: e.matmul(
                        ps[0:64, bank, 0:cw], lhsT=wt[0:kk, :], rhs=src[0:kk, c0:c0 + cw], start=True, stop=True),
                        r=[sk_ if li == 0 else (sk_ if li == 1 else "hB"), wk_], w=[("ps", bank)])
                    cx.op("dve", lambda e, li=li, cw=cw, bank=bank: e.tensor_scalar(
                        out=arg[:, 0:cw], in0=ps[0:64, bank, 0:cw], scalar1=bfr[:, li:li + 1], scalar2=bfr[:, 3 + li:4 + li],
                        op0=ALU.add, op1=ALU.mult), r=[("ps", bank), "bfr"], w=["arg"])
                    cx.op("dve", lambda e, cw=cw: e.tensor_scalar(out=msk[:, 0:cw], in0=arg[:, 0:cw], scalar1=PI, scalar2=-2.0 * PI,
                                                                 op0=ALU.is_gt, op1=ALU.mult), r=["arg"], w=["msk"])
                    cx.op("dve", lambda e, cw=cw: e.tensor_tensor(out=arg[:, 0:cw], in0=arg[:, 0:cw], in1=msk[:, 0:cw], op=ALU.add),
                          r=["arg", "msk"], w=["arg"])
                    cx.op("dve", lambda e, cw=cw: e.tensor_scalar(out=msk[:, 0:cw], in0=arg[:, 0:cw], scalar1=-PI, scalar2=2.0 * PI,
                                                                 op0=ALU.is_lt, op1=ALU.mult), r=["arg"], w=["msk"])
                    cx.op("dve", lambda e, cw=cw: e.tensor_tensor(out=arg[:, 0:cw], in0=arg[:, 0:cw], in1=msk[:, 0:cw], op=ALU.add),
                          r=["arg", "msk"], w=["arg"])
                    wkey = dk_ if li != 2 else "h3b"
                    cx.op("act", lambda e, dst=dst, c0=c0, cw=cw: e.activation(out=dst[:, c0:c0 + cw], in_=arg[:, 0:cw], func=AF.Sin),
                          r=["arg"], w=[wkey])
            cx.flush()
          with ExitStack() as st:
            def sb(name, shape, dt):
                return st.enter_context(nc.sbuf_tensor(self.uname(name), shape, dt))
            w4f = sb("f_w4f", [64, 4 * B_W], F32)
            w4 = sb("f_w4", [64, 4 * B_W], BF16)
            absd = sb("f_absd", [128, B_W], F32)
            negt = sb("f_negt", [128, NTs], F32)
            ones = sb("f_ones", [128, 128], BF16)
            hf = sb("f_hf", [128, NTs, 512], BF16)
            hb = sb("f_hb", [128, NTs, 512], BF16)
            hp = sb("f_hp", [128, NTs, 512], BF16)
            win = [sb("f_win%d" % i, [128, 512], F32) for i in range(2)]
            absf = [sb("f_absf%d" % i, [128, 512], BF16) for i in range(2)]
            absb = [sb("f_absb%d" % i, [128, 512], BF16) for i in range(2)]
            sf = sb("f_sf", [128, 512], F32)
            sbk = sb("f_sb", [128, 512], F32)
            Fb = [sb("f_F%d" % i, [128, NTs, 256], BF16) for i in range(2)]
            ko = [sb("f_ko%d" % i, [128, 2, 512], BF16) for i in range(2)]
            cx.dma("sp", lambda e: e.dma_start(out=w4f[:], in_=inp["hy_f_w4"][l]), "w4", w=["w4f"])
            cx.op("dve", lambda e: e.tensor_copy(out=w4[:], in_=w4f[:]), r=["w4f"], w=["w4"])
            cx.dma("sp", lambda e: e.dma_start(out=absd[:], in_=inp["absdelta"][0:1, :].partition_broadcast(128)), "absd", w=["absd"])
            cx.dma("sp", lambda e: e.dma_start(out=negt[:], in_=inp["negt%d" % L][:, :]), "negt", w=["negt"])
            cx.op("dve", lambda e: e.memset(ones[:], 1.0), w=["ones"])
            h3 = h3b
            kf = self.s_kf[L]
            fwdP = inp["fwd%d" % L]
            it = 0
            for o in range(2):
                for cb in range(2):
                    colf = o * 2048 + cb * 512
                    colb = o * 2048 + 1024 + cb * 512
                    for j in range(NTs):
                        q = j % 2
                        bf_, bb_ = (0, 1) if j % 2 == 0 else (4, 5)
                        cx.op("pe", lambda e, j=j, colf=colf, bf_=bf_: e.matmul(ps[:, bf_, :], lhsT=h3[:, j * 128:(j + 1) * 128],
                                                                               rhs=w4[:, colf:colf + 512], start=True, stop=True),
                              r=["h3b", "w4"], w=[("ps", bf_)])
                        cx.op("pe", lambda e, j=j, colb=colb, bb_=bb_: e.matmul(ps[:, bb_, :], lhsT=h3[:, j * 128:(j + 1) * 128],
                                                                               rhs=w4[:, colb:colb + 512], start=True, stop=True),
                              r=["h3b", "w4"], w=[("ps", bb_)])
                        cx.op("act", lambda e, j=j, cb=cb, q=q: e.activation(out=win[q][:], in_=absd[:, cb * 512:(cb + 1) * 512], func=AF.Exp,
                                                                            scale=negt[:, j:j + 1]), r=["absd", "negt"], w=[("win", q)])
                        cx.op("dve", lambda e, j=j, q=q, bf_=bf_: e.tensor_tensor(out=hf[:, j, :], in0=ps[:, bf_, :], in1=win[q][:], op=ALU.mult),
                              r=[("ps", bf_), ("win", q)], w=[("hf", j)])
                        cx.op("dve", lambda e, j=j, q=q, bb_=bb_: e.tensor_tensor(out=hb[:, j, :], in0=ps[:, bb_, :], in1=win[q][:], op=ALU.mult),
                              r=[("ps", bb_), ("win", q)], w=[("hb", j)])
                        if j == 0:
                            cx.op("dve", lambda e: e.memset(hb[0:1, 0, :], 0.0), w=[("hb", 0)])
                        cx.op("act", lambda e, j=j, q=q: e.activation(out=absf[q][:], in_=hf[:, j, :], func=AF.Abs),
                              r=[("hf", j)], w=[("absf", q)])
                        cx.op("act", lambda e, j=j, q=q: e.activation(out=absb[q][:], in_=hb[:, j, :], func=AF.Abs),
                              r=[("hb", j)], w=[("absb", q)])
                        cx.op("pe", lambda e, j=j, q=q: e.matmul(ps[:, 2, :], lhsT=ones[:], rhs=absf[q][:], start=(j == 0), stop=(j == NTs - 1)),
                              r=["ones", ("absf", q)], w=[("ps", 2)])
                        cx.op("pe", lambda e, j=j, q=q: e.matmul(ps[:, 3, :], lhsT=ones[:], rhs=absb[q][:], start=(j == 0), stop=(j == NTs - 1)),
                              r=["ones", ("absb", q)], w=[("ps", 3)])
                    cx.op("dve", lambda e: e.tensor_scalar_add(out=sf[:], in0=ps[:, 2, :], scalar1=FILTER_EPS), r=[("ps", 2)], w=["sf"])
                    cx.op("dve", lambda e: e.reciprocal(out=sf[:], in_=sf[:]), r=["sf"], w=["sf"])
                    cx.op("dve", lambda e: e.tensor_scalar_add(out=sbk[:], in0=ps[:, 3, :], scalar1=FILTER_EPS), r=[("ps", 3)], w=["sb"])
                    cx.op("dve", lambda e: e.reciprocal(out=sbk[:], in_=sbk[:]), r=["sb"], w=["sb"])
                    allf = [("hf", j) for j in range(NTs)]
                    allb = [("hb", j) for j in range(NTs)]
                    allp = [("hp", j) for j in range(NTs)]
                    J1 = max(1, (3 * NTs) // 4)
                    for (eng, ja, jb) in (("dve", 0, J1), ("pool", J1, NTs)):
                        if jb <= ja:
                            continue
                        nj = jb - ja
                        kf_ = [("hf", j) for j in range(ja, jb)]
                        kb_ = [("hb", j) for j in range(ja, jb)]
                        kp_ = [("hp", j) for j in range(ja, jb)]
                        cx.op(eng, lambda e, ja=ja, jb=jb, nj=nj: e.tensor_tensor(out=hf[:, ja:jb, :], in0=hf[:, ja:jb, :],
                                                                                 in1=sf[:].unsqueeze(1).to_broadcast([128, nj, 512]), op=ALU.mult),
                              r=kf_ + ["sf"], w=kf_)
                        cx.op(eng, lambda e, ja=ja, jb=jb, nj=nj: e.tensor_tensor(out=hb[:, ja:jb, :], in0=hb[:, ja:jb, :],
                                                                                 in1=sbk[:].unsqueeze(1).to_broadcast([128, nj, 512]), op=ALU.mult),
                              r=kb_ + ["sb"], w=kb_)
                        cx.op(eng, lambda e, ja=ja, jb=jb: e.tensor_tensor(out=hp[:, ja:jb, :], in0=hf[:, ja:jb, :], in1=hb[:, ja:jb, :], op=ALU.add),
                              r=kf_ + kb_, w=kp_)
                        cx.op(eng, lambda e, ja=ja, jb=jb: e.tensor_tensor(out=hf[:, ja:jb, :], in0=hf[:, ja:jb, :], in1=hb[:, ja:jb, :], op=ALU.subtract),
                              r=kf_ + kb_, w=kf_)
                    for i in range(NTs):
                        q = it % 2
                        it += 1
                        cx.dma("sp", lambda e, i=i, q=q: e.dma_start(out=Fb[q][:], in_=fwdP[i]), ("F", q), w=[("F", q)])
                        bre, bim = 4 + 2 * q, 5 + 2 * q
                        for j in range(NTs):
                            cx.op("pe", lambda e, j=j, q=q, bre=bre: e.matmul(ps[:, bre, :], lhsT=Fb[q][:, j, 0:128], rhs=hp[:, j, :],
                                                                             start=(j == 0), stop=(j == NTs - 1)),
                                  r=[("F", q), ("hp", j)], w=[("ps", bre)])
                            cx.op("pe", lambda e, j=j, q=q, bim=bim: e.matmul(ps[:, bim, :], lhsT=Fb[q][:, j, 128:256], rhs=hf[:, j, :],
                                                                             start=(j == 0), stop=(j == NTs - 1)),
                                  r=[("F", q), ("hf", j)], w=[("ps", bim)])
                        cx.op("act", lambda e, q=q, bre=bre: e.copy(out=ko[q][:, 0, :], in_=ps[:, bre, :]), r=[("ps", bre)], w=[("ko", q)])
                        cx.op("dve", lambda e, q=q, bim=bim: e.tensor_copy(out=ko[q][:, 1, :], in_=ps[:, bim, :]), r=[("ps", bim)], w=[("ko", q)])
                        cx.dma("act", lambda e, i=i, q=q, o=o, cb=cb: e.dma_start(
                            out=kf[2 * i * 128:(2 * i + 2) * 128, o, cb * 512:(cb + 1) * 512].rearrange("(c p) n -> p c n", p=128), in_=ko[q][:]),
                            ("ko", q), r=[("ko", q)])
            cx.flush()

    def phase_B(self, l):
        nc, cx, ps = self.nc, self.cx, self.ps
        inp = self.inp
        t0 = 0
        for L in self.seqs:
            NTs = L // 128
            fwdP = inp["fwd%d" % L]
            invP = inp["inv%d" % L]
            kf = self.s_kf[L]
            for cb in range(2):
                c0, c1 = cb * 512, (cb + 1) * 512
                with ExitStack() as st0:
                    zT = st0.enter_context(nc.sbuf_tensor(self.uname("b_zT"), [128, NTs, 512], BF16))
                    zv = self.s_zb.rearrange("t (a c) -> t a c", a=3)
                    cwv = inp["hy_conv_w"][l:l + 1].rearrange("o k (a c) -> o k a c", a=3)
                    cbv = inp["hy_conv_b"][l:l + 1, :].rearrange("o (a c) -> o a c", a=3)

                    def conv_chunk(j, part, tiles, cw_t, cb_t, out_ap, out_key, tag, wtag):
                        cur, prv, nxt, acc, tm = tiles
                        q = j % 2
                        r0 = t0 + j * 128
                        cx.dma("sp", lambda e: e.dma_start(out=cur[q][:], in_=zv[r0:r0 + 128, part, c0:c1]), (tag + "cur", q), w=[(tag + "cur", q)])
                        if j == 0:
                            cx.op("pool", lambda e: e.memset(prv[q][:], 0.0), w=[(tag + "prv", q)])
                            cx.dma("sp", lambda e: e.dma_start(out=prv[q][1:128], in_=zv[r0:r0 + 127, part, c0:c1]), (tag + "prv", q), w=[(tag + "prv", q)])
                        else:
                            cx.dma("sp", lambda e: e.dma_start(out=prv[q][:], in_=zv[r0 - 1:r0 + 127, part, c0:c1]), (tag + "prv", q), w=[(tag + "prv", q)])
                        if j == NTs - 1:
                            cx.op("pool", lambda e: e.memset(nxt[q][:], 0.0), w=[(tag + "nxt", q)])
                            cx.dma("sp", lambda e: e.dma_start(out=nxt[q][0:127], in_=zv[r0 + 1:r0 + 128, part, c0:c1]), (tag + "nxt", q), w=[(tag + "nxt", q)])
                        else:
                            cx.dma("sp", lambda e: e.dma_start(out=nxt[q][:], in_=zv[r0 + 1:r0 + 129, part, c0:c1]), (tag + "nxt", q), w=[(tag + "nxt", q)])
                        cx.op("pool", lambda e: e.tensor_tensor(out=acc[q][:], in0=cur[q][:], in1=cw_t[:, 1, :], op=ALU.mult),
                              r=[(tag + "cur", q), wtag + "cw"], w=[(tag + "acc", q)])
                        cx.op("pool", lambda e: e.tensor_tensor(out=tm[q][:], in0=prv[q][:], in1=cw_t[:, 0, :], op=ALU.mult),
                              r=[(tag + "prv", q), wtag + "cw"], w=[(tag + "tm", q)])
                        cx.op("dve", lambda e: e.tensor_tensor(out=acc[q][:], in0=acc[q][:], in1=tm[q][:], op=ALU.add),
                              r=[(tag + "acc", q), (tag + "tm", q)], w=[(tag + "acc", q)])
                        cx.op("pool", lambda e: e.tensor_tensor(out=tm[q][:], in0=nxt[q][:], in1=cw_t[:, 2, :], op=ALU.mult),
                              r=[(tag + "nxt", q), wtag + "cw"], w=[(tag + "tm", q)])
                        cx.op("dve", lambda e: e.tensor_tensor(out=acc[q][:], in0=acc[q][:], in1=tm[q][:], op=ALU.add),
                              r=[(tag + "acc", q), (tag + "tm", q)], w=[(tag + "acc", q)])
                        cx.op("dve", lambda e: e.tensor_tensor(out=out_ap, in0=acc[q][:], in1=cb_t, op=ALU.add),
                              r=[(tag + "acc", q), wtag + "cb"], w=[out_key])

                    with ExitStack() as st:
                        def sb(name, shape, dt):
                            return st.enter_context(nc.sbuf_tensor(self.uname(name), shape, dt))
                        cwt = sb("b_cw", [128, 3, 512], F32)
                        cbt = sb("b_cb", [128, 512], F32)
                        tiles = tuple([sb("b_%s%d" % (nm, i), [128, 512], dt) for i in range(2)]
                                      for nm, dt in (("cur", BF16), ("prv", BF16), ("nxt", BF16), ("acc", F32), ("tm", F32)))
                        cx.dma("sp", lambda e: e.dma_start(out=cwt[:], in_=cwv[:, :, 0, c0:c1].partition_broadcast(128)), "cw", w=["vcw"])
                        cx.dma("sp", lambda e: e.dma_start(out=cbt[:], in_=cbv[:, 0, c0:c1].partition_broadcast(128)), "cb", w=["vcb"])
                        for j in range(NTs):
                            conv_chunk(j, 0, tiles, cwt, cbt[:], zT[:, j, :], ("zT", j), "v", "v")
                        cx.flush()
                    with ExitStack() as st:
                        def sb(name, shape, dt):
                            return st.enter_context(nc.sbuf_tensor(self.uname(name), shape, dt))
                        Y = sb("b_Y", [128, 2 * NTs, 512], BF16)
                        SB = [sb("b_S%d" % i, [128, 2 * NTs * 128], BF16) for i in range(2)]
                        skp = sb("b_skip", [128, 2, 512], F32)
                        kft = [sb("b_kf%d" % i, [128, 2, 512], BF16) for i in range(2)]
                        tt = [sb("b_t%d" % i, [128, 512], F32) for i in range(4)]
                        xcw = [sb("b_xcw%d" % i, [128, 3, 512], F32) for i in range(2)]
                        xcb = [sb("b_xcb%d" % i, [128, 512], F32) for i in range(2)]
                        xtiles = tuple([sb("b_x%s%d" % (nm, i), [128, 512], dt) for i in range(2)]
                                       for nm, dt in (("cur", BF16), ("prv", BF16), ("nxt", BF16), ("acc", F32), ("tm", F32)))
                        xg = [sb("b_xg%d" % i, [128, 512], F32) for i in range(2)]
                        g1 = [sb("b_g1%d" % i, [128, 512], F32) for i in range(2)]
                        g2 = [sb("b_g2%d" % i, [128, 512], F32) for i in range(2)]
                        yst = [sb("b_yst%d" % i, [128, 512], BF16) for i in range(2)]
                        cx.dma("sp", lambda e: e.dma_start(out=skp[:], in_=inp["hy_skip"][l:l + 1, :, c0:c1].partition_broadcast(128)), "skip", w=["skip"])
                        for o in range(2):
                            cx.dma("sp", lambda e, o=o: e.dma_start(out=xcw[o][:], in_=cwv[:, :, 1 + o, c0:c1].partition_broadcast(128)), ("xcw", o), w=["x%dcw" % o])
                            cx.dma("sp", lambda e, o=o: e.dma_start(out=xcb[o][:], in_=cbv[:, 1 + o, c0:c1].partition_broadcast(128)), ("xcb", o), w=["x%dcb" % o])
                        it = 0
                        for o in range(2):
                            for i in range(NTs):
                                q = it % 2
                                it += 1
                                Fq = SB[q][:, 0:NTs * 256].rearrange("p (j c) -> p j c", c=256)
                                cx.dma("sp", lambda e, Fq=Fq, i=i: e.dma_start(out=Fq, in_=fwdP[i]), ("S", q), w=[("S", q)])
                                cx.dma("sp", lambda e, q=q, i=i, o=o: e.dma_start(
                                    out=kft[q][:], in_=kf[2 * i * 128:(2 * i + 2) * 128, o, c0:c1].rearrange("(c p) n -> p c n", p=128)),
                                    ("kf", q), w=[("kf", q)])
                                bre, bim = 2 * q, 2 * q + 1
                                for j in range(NTs):
                                    cx.op("pe", lambda e, Fq=Fq, j=j, bre=bre: e.matmul(ps[:, bre, :], lhsT=Fq[:, j, 0:128], rhs=zT[:, j, :],
                                                                                       start=(j == 0), stop=(j == NTs - 1)),
                                          r=[("S", q), ("zT", j)], w=[("ps", bre)])
                                    cx.op("pe", lambda e, Fq=Fq, j=j, bim=bim: e.matmul(ps[:, bim, :], lhsT=Fq[:, j, 128:256], rhs=zT[:, j, :],
                                                                                       start=(j == 0), stop=(j == NTs - 1)),
                                          r=[("S", q), ("zT", j)], w=[("ps", bim)])
                                cx.op("dve", lambda e, q=q, bre=bre: e.tensor_tensor(out=tt[0][:], in0=ps[:, bre, :], in1=kft[q][:, 0, :], op=ALU.mult),
                                      r=[("ps", bre), ("kf", q)], w=[("tt", 0)])
                                cx.op("dve", lambda e, q=q, bim=bim: e.tensor_tensor(out=tt[1][:], in0=ps[:, bim, :], in1=kft[q][:, 1, :], op=ALU.mult),
                                      r=[("ps", bim), ("kf", q)], w=[("tt", 1)])
                                cx.op("dve", lambda e, q=q, bre=bre: e.tensor_tensor(out=tt[2][:], in0=ps[:, bre, :], in1=kft[q][:, 1, :], op=ALU.mult),
                                      r=[("ps", bre), ("kf", q)], w=[("tt", 2)])
                                cx.op("dve", lambda e, q=q, bim=bim: e.tensor_tensor(out=tt[3][:], in0=ps[:, bim, :], in1=kft[q][:, 0, :], op=ALU.mult),
                                      r=[("ps", bim), ("kf", q)], w=[("tt", 3)])
                                cx.op("pool", lambda e, i=i: e.tensor_tensor(out=Y[:, 2 * i, :], in0=tt[0][:], in1=tt[1][:], op=ALU.subtract),
                                      r=[("tt", 0), ("tt", 1)], w=[("Y", 2 * i)])
                                cx.op("pool", lambda e, i=i: e.tensor_tensor(out=Y[:, 2 * i + 1, :], in0=tt[2][:], in1=tt[3][:], op=ALU.add),
                                      r=[("tt", 2), ("tt", 3)], w=[("Y", 2 * i + 1)])
                            for j in range(NTs):
                                q = it % 2
                                it += 1
                                r0 = t0 + j * 128
                                Iq = SB[q][:, :].rearrange("p (r t) -> p r t", t=128)
                                cx.dma("sp", lambda e, Iq=Iq, j=j: e.dma_start(out=Iq, in_=invP[j]), ("S", q), w=[("S", q)])
                                jq = j % 2
                                conv_chunk(j, 1 + o, xtiles, xcw[o], xcb[o][:], xg[jq][:], ("xg", jq), "x", "x%d" % o)
                                by = 4 + q
                                for r in range(2 * NTs):
                                    cx.op("pe", lambda e, Iq=Iq, r=r, by=by: e.matmul(ps[:, by, :], lhsT=Iq[:, r, :], rhs=Y[:, r, :],
                                                                                     start=(r == 0), stop=(r == 2 * NTs - 1)),
                                          r=[("S", q), ("Y", r)], w=[("ps", by)])
                                cx.op("pool", lambda e, q=q, j=j, o=o: e.tensor_tensor(out=g1[q][:], in0=zT[:, j, :], in1=skp[:, o, :], op=ALU.mult),
                                      r=[("zT", j), "skip"], w=[("g1", q)])
                                cx.op("dve", lambda e, q=q, by=by: e.tensor_tensor(out=g2[q][:], in0=ps[:, by, :], in1=g1[q][:], op=ALU.add),
                                      r=[("ps", by), ("g1", q)], w=[("g2", q)])
                                if o == 0:
                                    cx.op("dve", lambda e, q=q, j=j, jq=jq: e.tensor_tensor(out=zT[:, j, :], in0=g2[q][:], in1=xg[jq][:], op=ALU.mult),
                                          r=[("g2", q), ("xg", jq)], w=[("zT", j)])
                                else:
                                    cx.op("dve", lambda e, q=q, jq=jq: e.tensor_tensor(out=yst[q][:], in0=g2[q][:], in1=xg[jq][:], op=ALU.mult),
                                          r=[("g2", q), ("xg", jq)], w=[("yst", q)])
                                    cx.dma("act", lambda e, q=q, r0=r0: e.dma_start(out=self.s_yb[r0:r0 + 128, c0:c1], in_=yst[q][:]), ("yst", q), r=[("yst", q)])
                        cx.flush()
            t0 += L

    def phase_C(self, l):
        self.phase_C1(l)
        self.phase_C2(l)

    def phase_C1(self, l):
        nc, cx, ps = self.nc, self.cx, self.ps
        with ExitStack() as st:
            def sb(name, shape, dt):
                return st.enter_context(nc.sbuf_tensor(self.uname(name), shape, dt))
            wbo = sb("c1_wbo", [128, 8, 1024], BF16)
            woo = sb("c1_woo", [128, 8, 1024], BF16)
            xts = [sb("c1_x%d" % i, [128, D], F32) for i in range(2)]
            ybs = [sb("c1_yb%d" % i, [128, D], BF16) for i in range(2)]
            gbs = [sb("c1_gb%d" % i, [128, D], BF16) for i in range(2)]
            gys = [sb("c1_gy%d" % i, [128, D], BF16) for i in range(2)]
            ybTs = [sb("c1_ybT%d" % i, [128, 8, 128], BF16) for i in range(2)]
            t32 = sb("c1_t32", [128, D], F32)
            m = sb("c1_m", [128, D], BF16)
            mT = sb("c1_mT", [128, 8, 128], BF16)
            cx.dma("sp", lambda e: e.dma_start(out=wbo[:], in_=self.s_wbo[l].rearrange("(j p) n -> p j n", p=128)), "wbo", w=["wbo"])
            cx.dma("sp", lambda e: e.dma_start(out=woo[:], in_=self.s_wo[l].rearrange("(j p) n -> p j n", p=128)), "woo", w=["woo"])
            src = self.inp["x"] if l == 0 else self.s_xres

            def Ca(i):
                p = i % 2
                r0 = i * 128
                xt, yb, gb, gy = xts[p], ybs[p], gbs[p], gys[p]
                cx.dma("sp", lambda e: e.dma_start(out=xt[:], in_=src[r0:r0 + 128, :]), ("x", p), w=[("x", p)])
                cx.dma("sp", lambda e: e.dma_start(out=yb[:], in_=self.s_yb[r0:r0 + 128, :]), ("yb", p), w=[("yb", p)])
                cx.dma("sp", lambda e: e.dma_start(out=gb[:], in_=self.s_gb[r0:r0 + 128, :]), ("gb", p), w=[("gb", p)])
                cx.dma("sp", lambda e: e.dma_start(out=gy[:], in_=self.s_gaya[r0:r0 + 128, :]), ("gy", p), w=[("gy", p)])
                bank = 2 + p
                pv = self.psb(bank)
                for k in range(8):
                    cx.op("pe", lambda e, k=k: e.transpose(out=pv[:, k * 128:(k + 1) * 128], in_=yb[:, k * 128:(k + 1) * 128],
                                                          identity=self.identb[:]), r=[("yb", p)], w=[("ps", bank)])
                cx.op("act", lambda e: e.copy(out=ybTs[p][:].rearrange("p k t -> p (k t)"), in_=pv[:, :]), r=[("ps", bank)], w=[("ybT", p)])

            def Cb(i, pend):
                p = i % 2
                r0 = i * 128
                xt, yb, gb, gy = xts[p], ybs[p], gbs[p], gys[p]
                ybT = ybTs[p]
                for nb in range(2):
                    for k in range(8):
                        cx.op("pe", lambda e, nb=nb, k=k: e.matmul(ps[:, nb, :], lhsT=ybT[:, k, :], rhs=wbo[:, k, nb * 512:(nb + 1) * 512],
                                                                  start=(k == 0), stop=(k == 7)), r=[("ybT", p), "wbo"], w=[("ps", nb)])
                    cx.op("dve", lambda e, nb=nb: e.tensor_tensor(out=t32[:, nb * 512:(nb + 1) * 512], in0=ps[:, nb, :],
                                                                 in1=gb[:, nb * 512:(nb + 1) * 512], op=ALU.mult),
                          r=[("ps", nb), ("gb", p)], w=[("t32", nb)])
                    cx.op("pool", lambda e, nb=nb: e.tensor_tensor(out=m[:, nb * 512:(nb + 1) * 512], in0=t32[:, nb * 512:(nb + 1) * 512],
                                                                  in1=gy[:, nb * 512:(nb + 1) * 512], op=ALU.add),
                          r=[("t32", nb), ("gy", p)], w=[("m", nb)])
                cx.replay(pend, len(pend))
                pv3 = self.psb(6)
                for k in range(8):
                    cx.op("pe", lambda e, k=k: e.transpose(out=pv3[:, k * 128:(k + 1) * 128], in_=m[:, k * 128:(k + 1) * 128],
                                                          identity=self.identb[:]), r=[("m", k // 4)], w=[("ps", 6)])
                cx.op("act", lambda e: e.copy(out=mT[:].rearrange("p k t -> p (k t)"), in_=pv3[:, :]), r=[("ps", 6)], w=["mT"])
                for nb in range(2):
                    bank = 4 + nb
                    for k in range(8):
                        cx.op("pe", lambda e, nb=nb, k=k, bank=bank: e.matmul(ps[:, bank, :], lhsT=mT[:, k, :], rhs=woo[:, k, nb * 512:(nb + 1) * 512],
                                                                             start=(k == 0), stop=(k == 7)), r=["mT", "woo"], w=[("ps", bank)])
                    cx.op("dve", lambda e, nb=nb, bank=bank: e.tensor_tensor(out=xt[:, nb * 512:(nb + 1) * 512], in0=ps[:, bank, :],
                                                                            in1=xt[:, nb * 512:(nb + 1) * 512], op=ALU.add),
                          r=[("ps", bank), ("x", p)], w=[("x", p)])
                cx.dma("pool", lambda e: e.dma_start(out=self.s_xres[r0:r0 + 128, :], in_=xt[:]), ("xo", p), r=[("x", p)])

            Ca(0)
            for i in range(self.NT):
                pend = []
                if i + 1 < self.NT:
                    cx.defer_begin()
                    Ca(i + 1)
                    pend = cx.defer_end()
                Cb(i, pend)
            cx.flush()

    def phase_C2(self, l):
        nc, cx, ps = self.nc, self.cx, self.ps
        last = (l == DEPTH - 1)
        RG = 10
        with ExitStack() as st:
            def sb(name, shape, dt):
                return st.enter_context(nc.sbuf_tensor(self.uname(name), shape, dt))
            inp = self.inp
            keysT = sb("c2_keysT", [128, 16, 128], BF16)
            with ExitStack() as st2:
                kn = st2.enter_context(nc.sbuf_tensor(self.uname("c2_kn"), [128, 16, 128], F32))
                knb = st2.enter_context(nc.sbuf_tensor(self.uname("c2_knb"), [128, 16, 128], BF16))
                cx.dma("sp", lambda e: e.dma_start(out=kn[:], in_=inp["peer_keys"][l].rearrange("h t n d -> n (h t) d")), "kn", w=["kn"])
                cx.op("dve", lambda e: e.tensor_copy(out=knb[:], in_=kn[:]), r=["kn"], w=["knb"])
                for half in range(2):
                    pv = self.psb(half)
                    for j in range(8):
                        g = half * 8 + j
                        cx.op("pe", lambda e, j=j, g=g, pv=pv: e.transpose(out=pv[:, j * 128:(j + 1) * 128], in_=knb[:, g, :], identity=self.identb[:]),
                              r=["knb"], w=[("ps", half)])
                    cx.op("act", lambda e, pv=pv, half=half: e.copy(out=keysT[:, half * 8:half * 8 + 8, :].rearrange("p g n -> p (g n)"), in_=pv[:, :]),
                          r=[("ps", half)], w=["keysT"])
                cx.flush()
            wq = sb("c2_wq", [128, 8, 2048], BF16)
            gffn = sb("c2_gffn", [128, D], F32)
            gfin = sb("c2_gfin", [128, D], F32) if last else None
            iota = sb("c2_iota", [128, 16], F32)
            thr = sb("c2_thr", [128, 16], F32)
            amask = sb("c2_amask", [128, 128, 128], BF16)
            xts = [sb("c2_x%d" % i, [128, D], F32) for i in range(2)]
            tl = dict(junk=sb("c2_junk", [128, D], BF16), ss=sb("c2_ss", [128, 1], F32), rstd=sb("c2_rstd", [128, 1], F32),
                      hn=sb("c2_hn", [128, D], BF16), hT=sb("c2_hT", [128, 8, 128], BF16))
            qT = sb("c2_qT", [128, 16, 128], BF16)
            sc = sb("c2_s", [128, 16, 128], F32)
            sc2 = sb("c2_s2", [128, 16, 128], F32)
            tv = sb("c2_tv", [128, 16, 16], F32)
            ti = sb("c2_ti", [128, 16, 16], U32)
            tif = sb("c2_tif", [128, 16, 16], BF16)
            cand = sb("c2_cand", [128, 8, 256], F32)
            cand2 = sb("c2_cand2", [128, 8, 256], F32)
            bs = sb("c2_bs", [128, 8, 16], F32)
            bp = sb("c2_bp", [128, 8, 16], U32)
            pau = sb("c2_pau", [128, 8, 16], U32)
            pbu = sb("c2_pbu", [128, 8, 16], U32)
            oh = sb("c2_oh", [128, 8, 16, 16], BF16)
            paf = sb("c2_paf", [128, 8, 16], BF16)
            pbf = sb("c2_pbf", [128, 8, 16], BF16)
            iotab = sb("c2_iotab", [128, 16], BF16)
            i1 = sb("c2_i1", [128, 8, 16], F32)
            i2 = sb("c2_i2", [128, 8, 16], F32)
            idxf = sb("c2_idxf", [128, 128], F32)
            ee = sb("c2_e", [128, 8, 16], F32)
            zz = sb("c2_z", [128, 8], F32)
            gate = sb("c2_gate", [128, 8, 16], F32)
            idxTs = [sb("c2_idxT%d" % i, [128, 128], U32) for i in range(2)]
            gateTs = [sb("c2_gateT%d" % i, [128, 128], F32) for i in range(2)]
            hns = [sb("c2_hn%d" % i, [128, D], BF16) for i in range(2)]
            hTt = sb("c2_hTt", [128, 128], F32)
            gl = sb("c2_gl", [128, 128], F32)
            aT = sb("c2_aT", [128, 128], BF16)
            Gr = [sb("c2_G%d" % i, [128, 2 * D], BF16) for i in range(RG)]
            junk2 = sb("c2_junk2", [128, D], BF16)
            ost = [sb("c2_ost%d" % i, [128, D], F32) for i in range(2)]
            fin = dict(junk=tl["junk"], ss=sb("c2_fss", [128, 1], F32), rstd=sb("c2_frstd", [128, 1], F32))

            inp = self.inp
            cx.dma("sp", lambda e: e.dma_start(out=wq[:], in_=self.s_wq[l].rearrange("(j p) n -> p j n", p=128)), "wq", w=["wq"])
            cx.dma("sp", lambda e: e.dma_start(out=gffn[:], in_=inp["norm_ffn"][l:l + 1, :].partition_broadcast(128)), "g", w=["g"])
            if last:
                cx.dma("sp", lambda e: e.dma_start(out=gfin[:], in_=inp["final_norm"].rearrange("(o n) -> o n", o=1).partition_broadcast(128)),
                       "gfin", w=["gfin"])
            cx.dma("sp", lambda e: e.dma_start(out=iota[:], in_=inp["iota16"][:, :]), "iota", w=["iota"])
            cx.op("dve", lambda e: e.tensor_copy(out=iotab[:], in_=iota[:]), r=["iota"], w=["iotab"])
            cx.op("pool", lambda e: e.memset(amask[:], 0.0), w=["amask"])
            uv_tab = self.s_uv[l].rearrange("e two d -> e (two d)")
            amf = amask[:].rearrange("p t m -> p (t m)")
            def front(i):
                p = i % 2
                r0 = i * 128
                xt = xts[p]
                cx.dma("sp", lambda e, xt=xt, r0=r0: e.dma_start(out=xt[:], in_=self.s_xres[r0:r0 + 128, :]), ("x", p), w=[("x", p)])
                tlp = dict(tl); tlp["hn"] = hns[p]
                self.rms_to_hT(xt, ("x", p), gffn, tlp, bank=0, hn_out_key=("hn", p))
                xn, xnT = hns[p], tl["hT"]
                idxT, gateT = idxTs[p], gateTs[p]
                for rnd in range(2):
                    for g8 in range(8):
                        hp = rnd * 8 + g8
                        bank = g8 // 4
                        for k in range(8):
                            cx.op("pe", lambda e, hp=hp, g8=g8, bank=bank, k=k: e.matmul(
                                ps[:, bank, (g8 % 4) * 128:(g8 % 4 + 1) * 128], lhsT=wq[:, k, hp * 128:(hp + 1) * 128], rhs=xnT[:, k, :],
                                start=(k == 0), stop=(k == 7)), r=["wq", "hT"], w=[("ps", bank)])
                    cx.op("act", lambda e, rnd=rnd: e.copy(out=qT[:, rnd * 8:rnd * 8 + 8, :].rearrange("p g t -> p (g t)"),
                                                          in_=ps[:, 0:2, :].rearrange("p a n -> p (a n)")),
                          r=[("ps", 0), ("ps", 1)], w=[("qT", rnd)])
                for rnd in range(2):
                    for g8 in range(8):
                        hp = rnd * 8 + g8
                        bank = g8 // 4
                        cx.op("pe", lambda e, hp=hp, g8=g8, bank=bank: e.matmul(
                            ps[:, bank, (g8 % 4) * 128:(g8 % 4 + 1) * 128], lhsT=qT[:, hp, :], rhs=keysT[:, hp, :], start=True, stop=True),
                            r=[("qT", rnd), "keysT"], w=[("ps", bank)])
                    cx.op("act", lambda e, rnd=rnd: e.copy(out=sc[:, rnd * 8:rnd * 8 + 8, :].rearrange("p g n -> p (g n)"),
                                                          in_=ps[:, 0:2, :].rearrange("p a n -> p (a n)")),
                          r=[("ps", 0), ("ps", 1)], w=[("sc", rnd)])
                for g in range(16):
                    rk = ("sc", g // 8)
                    cx.op("dve", lambda e, g=g: e.max(out=tv[:, g, 0:8], in_=sc[:, g, :]), r=[rk], w=[("tv", g)])
                for g in range(16):
                    rk = ("sc", g // 8)
                    cx.op("dve", lambda e, g=g: e.max_index(out=ti[:, g, 0:8], in_max=tv[:, g, 0:8], in_values=sc[:, g, :]),
                          r=[rk, ("tv", g)], w=[("ti", g)])
                for g in range(16):
                    rk = ("sc", g // 8)
                    cx.op("dve", lambda e, g=g: e.match_replace(out=sc2[:, g, :], in_to_replace=tv[:, g, 0:8], in_values=sc[:, g, :], imm_value=-1e30),
                          r=[rk, ("tv", g)], w=[("sc2", g)])
                for g in range(16):
                    cx.op("dve", lambda e, g=g: e.max(out=tv[:, g, 8:16], in_=sc2[:, g, :]), r=[("sc2", g)], w=[("tv", g)])
                for g in range(16):
                    cx.op("dve", lambda e, g=g: e.max_index(out=ti[:, g, 8:16], in_max=tv[:, g, 8:16], in_values=sc2[:, g, :]),
                          r=[("sc2", g), ("tv", g)], w=[("ti", g)])
                alltv = [("tv", g) for g in range(16)]
                allti = [("ti", g) for g in range(16)]
                cx.op("dve", lambda e: e.tensor_copy(out=tif[:], in_=ti[:]), r=allti, w=["tif"])
                tv4 = tv[:].rearrange("p (h t) k -> p h t k", t=2)
                tif4 = tif[:].rearrange("p (h t) k -> p h t k", t=2)
                cand4 = cand[:].rearrange("p h (a b) -> p h a b", b=16)
                cx.op("dve", lambda e: e.tensor_tensor(out=cand4, in0=tv4[:, :, 0, :].unsqueeze(3).to_broadcast([128, 8, 16, 16]),
                                                       in1=tv4[:, :, 1, :].unsqueeze(2).to_broadcast([128, 8, 16, 16]), op=ALU.add),
                      r=alltv, w=["cand"])
                for h in range(8):
                    cx.op("dve", lambda e, h=h: e.max(out=bs[:, h, 0:8], in_=cand[:, h, :]), r=["cand"], w=[("bs", h)])
                for h in range(8):
                    cx.op("dve", lambda e, h=h: e.max_index(out=bp[:, h, 0:8], in_max=bs[:, h, 0:8], in_values=cand[:, h, :]),
                          r=["cand", ("bs", h)], w=[("bp", h)])
                for h in range(8):
                    cx.op("dve", lambda e, h=h: e.match_replace(out=cand2[:, h, :], in_to_replace=bs[:, h, 0:8], in_values=cand[:, h, :], imm_value=-1e30),
                          r=["cand", ("bs", h)], w=[("cand2", h)])
                for h in range(8):
                    cx.op("dve", lambda e, h=h: e.max(out=bs[:, h, 8:16], in_=cand2[:, h, :]), r=[("cand2", h)], w=[("bs", h)])
                for h in range(8):
                    cx.op("dve", lambda e, h=h: e.max_index(out=bp[:, h, 8:16], in_max=bs[:, h, 8:16], in_values=cand2[:, h, :]),
                          r=[("cand2", h), ("bs", h)], w=[("bp", h)])
                allbs = [("bs", h) for h in range(8)]
                allbp = [("bp", h) for h in range(8)]
                B4 = [128, 8, 16, 16]
                cx.op("dve", lambda e: e.tensor_single_scalar(out=pau[:], in_=bp[:], scalar=4, op=ALU.logical_shift_right), r=allbp, w=["pau"])
                cx.op("dve", lambda e: e.tensor_single_scalar(out=pbu[:], in_=bp[:], scalar=15, op=ALU.bitwise_and), r=allbp, w=["pbu"])
                cx.op("dve", lambda e: e.tensor_copy(out=paf[:], in_=pau[:]), r=["pau"], w=["paf"])
                cx.op("dve", lambda e: e.tensor_copy(out=pbf[:], in_=pbu[:]), r=["pbu"], w=["pbf"])
                for (pf, pk, tsel, iout, ik) in ((paf, "paf", 0, i1, "i1"), (pbf, "pbf", 1, i2, "i2")):
                    cx.op("dve", lambda e, pf=pf: e.tensor_tensor(out=oh[:], in0=pf[:].unsqueeze(3).to_broadcast(B4),
                                                                 in1=iotab[:].unsqueeze(1).unsqueeze(1).to_broadcast(B4), op=ALU.is_equal),
                          r=[pk, "iotab"], w=["oh"])
                    cx.op("dve", lambda e, tsel=tsel: e.tensor_tensor(out=oh[:], in0=oh[:], in1=tif4[:, :, tsel, :].unsqueeze(2).to_broadcast(B4),
                                                                     op=ALU.mult), r=["oh", "tif"], w=["oh"])
                    cx.op("dve", lambda e, iout=iout: e.tensor_reduce(out=iout[:], in_=oh[:], axis=AX.X, op=ALU.add), r=["oh"], w=[ik])
                cx.op("dve", lambda e: e.scalar_tensor_tensor(out=idxf[:], in0=i1[:].rearrange("p h k -> p (h k)"), scalar=128.0,
                                                              in1=i2[:].rearrange("p h k -> p (h k)"), op0=ALU.mult, op1=ALU.add),
                      r=["i1", "i2"], w=["idxf"])
                cx.op("dve", lambda e: e.tensor_tensor(out=ee[:], in0=bs[:], in1=bs[:, :, 0:1].to_broadcast([128, 8, 16]), op=ALU.subtract),
                      r=allbs, w=["ee"])
                cx.op("act", lambda e: e.activation(out=ee[:], in_=ee[:], func=AF.Exp), r=["ee"], w=["ee"])
                cx.op("dve", lambda e: e.tensor_reduce(out=zz[:], in_=ee[:], axis=AX.X, op=ALU.add), r=["ee"], w=["zz"])
                cx.op("dve", lambda e: e.reciprocal(out=zz[:], in_=zz[:]), r=["zz"], w=["zz"])
                cx.op("dve", lambda e: e.tensor_tensor(out=gate[:], in0=ee[:], in1=zz[:].unsqueeze(2).to_broadcast([128, 8, 16]), op=ALU.mult),
                      r=["ee", "zz"], w=["gate"])
                cx.op("pe", lambda e: e.transpose(out=ps[:, 0, 0:128], in_=idxf[:], identity=self.identf[:]), r=["idxf"], w=[("ps", 0)])
                cx.op("dve", lambda e: e.tensor_copy(out=idxT[:], in_=ps[:, 0, 0:128]), r=[("ps", 0)], w=[("idxT", p)])
                cx.op("pe", lambda e: e.transpose(out=ps[:, 1, 0:128], in_=gate[:].rearrange("p h k -> p (h k)"), identity=self.identf[:]),
                      r=["gate"], w=[("ps", 1)])
                cx.op("act", lambda e: e.copy(out=gateT[:], in_=ps[:, 1, 0:128]), r=[("ps", 1)], w=[("gateT", p)])
                LAG = 2

            def tokloop(i, pending):
                p = i % 2
                r0 = i * 128
                xt = xts[p]
                xn = hns[p]
                idxT, gateT = idxTs[p], gateTs[p]
                nrep = (len(pending) + 111) // 112 if pending else 0
                def gather(t):
                    sg = t % RG
                    cx.dma("pool", lambda e, sg=sg, t=t: e.indirect_dma_start(
                        out=Gr[sg][:], out_offset=None, in_=uv_tab, in_offset=bass.IndirectOffsetOnAxis(ap=idxT[:, t:t + 1], axis=0)),
                        ("G", sg), r=[("idxT", p)], w=[("G", sg)])

                def udot(t):
                    sg = t % RG
                    bb = 4 + 2 * (t % 2)
                    for nb in range(2):
                        cx.op("pe", lambda e, t=t, nb=nb, bb=bb: e.matmul(ps[:, bb + nb, :], lhsT=self.identb[:, t:t + 1].to_broadcast([128, 128]),
                                                                         rhs=xn[:, nb * 512:(nb + 1) * 512], start=True, stop=True),
                              r=[("hn", p), "identb"], w=[("ps", bb + nb)])
                    cx.op("dve", lambda e, t=t, sg=sg, bb=bb: e.scalar_tensor_tensor(
                        out=junk2[:], in0=Gr[sg][:, 0:D], scalar=1.0, in1=ps[:, bb:bb + 2, :].rearrange("p a n -> p (a n)"),
                        op0=ALU.mult, op1=ALU.mult, accum_out=hTt[:, t:t + 1]),
                        r=[("G", sg), ("ps", bb), ("ps", bb + 1)], w=["junk2", ("hTt", t)])
                    cx.op("act", lambda e, t=t: e.activation(out=gl[:, t:t + 1], in_=hTt[:, t:t + 1], func=AF.Gelu), r=[("hTt", t)], w=[("gl", t)])

                def acol(t):
                    cx.op("pool", lambda e, t=t: e.tensor_tensor(out=amask[:, t, t:t + 1], in0=gl[:, t:t + 1], in1=gateT[:, t:t + 1], op=ALU.mult),
                          r=[("gl", t), ("gateT", p)], w=[("am", t)])

                def vmm(t):
                    sg = t % RG
                    for nb in range(2):
                        cx.op("pe", lambda e, t=t, nb=nb, sg=sg: e.matmul(ps[:, 2 + nb, :], lhsT=amask[:, t, :],
                                                                         rhs=Gr[sg][:, D + nb * 512:D + (nb + 1) * 512],
                                                                         start=(t == 0), stop=(t == 127)),
                              r=[("am", t), "amask", ("G", sg)], w=[("ps", 2 + nb)])

                LAG = 4
                for t in range(RG):
                    gather(t)
                for t in range(128 + LAG):
                    if t < 128:
                        udot(t)
                    if 1 <= t <= 128:
                        acol(t - 1)
                    if t >= LAG:
                        vmm(t - LAG)
                        if t - LAG + RG < 128:
                            gather(t - LAG + RG)
                    if pending and 8 <= t < 120:
                        cx.replay(pending, nrep)
                ot = ost[p]
                for nb in range(2):
                    cx.op("dve", lambda e, nb=nb, ot=ot, xt=xt: e.tensor_tensor(out=ot[:, nb * 512:(nb + 1) * 512], in0=ps[:, 2 + nb, :],
                                                                               in1=xt[:, nb * 512:(nb + 1) * 512], op=ALU.add),
                          r=[("ps", 2 + nb), ("x", p)], w=[("ost", p)])
                if not last:
                    cx.dma("act", lambda e, ot=ot, r0=r0: e.dma_start(out=self.s_xres[r0:r0 + 128, :], in_=ot[:]), ("ost", p), r=[("ost", p)])
                else:
                    cx.op("act", lambda e, ot=ot: e.activation(out=fin["junk"][:], in_=ot[:], func=AF.Square, accum_out=fin["ss"][:, 0:1]),
                          r=[("ost", p)], w=["junk", "fss"])
                    cx.op("dve", lambda e: e.tensor_scalar(out=fin["rstd"][:], in0=fin["ss"][:], scalar1=1.0 / D, scalar2=RMS_EPS,
                                                           op0=ALU.mult, op1=ALU.add), r=["fss"], w=["frstd"])
                    cx.op("act", lambda e: e.sqrt(out=fin["rstd"][:], in_=fin["rstd"][:]), r=["frstd"], w=["frstd"])
                    cx.op("dve", lambda e: e.reciprocal(out=fin["rstd"][:], in_=fin["rstd"][:]), r=["frstd"], w=["frstd"])
                    cx.op("dve", lambda e, ot=ot: e.scalar_tensor_tensor(out=ot[:], in0=ot[:], scalar=fin["rstd"][:, 0:1], in1=gfin[:],
                                                                        op0=ALU.mult, op1=ALU.mult), r=[("ost", p), "frstd", "gfin"], w=[("ost", p)])
                    cx.dma("act", lambda e, ot=ot, r0=r0: e.dma_start(out=self.y[r0:r0 + 128, :], in_=ot[:]), ("ost", p), r=[("ost", p)])
            cx.defer_begin()
            front(0)
            pend = cx.defer_end()
            cx.replay(pend, len(pend))
            for i in range(self.NT):
                pend = []
                if i + 1 < self.NT:
                    cx.defer_begin()
                    front(i + 1)
                    pend = cx.defer_end()
                tokloop(i, pend)
                cx.replay(pend, len(pend))
            cx.flush()


def _flat128(ap):
    nd = len(ap.shape)
    names = " ".join("d%d" % i for i in range(nd))
    flat = ap.rearrange("%s -> (%s)" % (names, names))
    return flat.rearrange("(p n) -> p n", p=128)


FULL_SEQS = (2048, 4096, 4096)
_CACHE = {}


def const_inputs(Ls):
    c = {
        "ident_b": np.eye(128, dtype=np.float32).astype(ml_dtypes.bfloat16),
        "ident_f": np.eye(128, dtype=np.float32),
        "absdelta": abs_deltas(),
        "iota16": np.tile(np.arange(16, dtype=np.float32)[None, :], (128, 1)),
    }
    for L in Ls:
        fwd, inv = dft_mats(L)
        featT, negt = filter_feats(L)
        c["fwd%d" % L] = fwd
        c["inv%d" % L] = inv
        c["featT%d" % L] = featT
        c["negt%d" % L] = negt
    return c


def kernel(**inputs):
    xp = np.asarray(inputs["x_prompt"], dtype=np.float32)
    xs = np.asarray(inputs["x_sample"], dtype=np.float32)
    n = 8
    prog = Prog(FULL_SEQS)
    nc = prog.build()
    consts = const_inputs(prog.Ls)
    weights = {nm: np.ascontiguousarray(np.asarray(inputs[nm], dtype=np.float32)) for nm, _ in WEIGHT_SPECS}
    in_maps = []
    for c in range(n):
        xcat = np.concatenate([xp[c], xs[2 * c], xs[2 * c + 1]], axis=0)
        m = {"x": np.ascontiguousarray(xcat)}
        m.update(weights)
        m.update(consts)
        in_maps.append(m)
    res = run_bass_kernel_spmd(nc, in_maps, core_ids=list(range(n)))
    yp = np.empty_like(xp)
    ys = np.empty_like(xs)
    for c in range(n):
        y = res.results[c]["y"]
        yp[c] = y[0:2048]
        ys[2 * c] = y[2048:2048 + 4096]
        ys[2 * c + 1] = y[2048 + 4096:]
    return (yp, ys)
```

```python
import math
from contextlib import ExitStack

import numpy as np
import ml_dtypes

import concourse.bass as bass
import concourse.mybir as mybir
from concourse.bass_utils import run_bass_kernel_spmd

F32 = mybir.dt.float32
BF16 = mybir.dt.bfloat16
U32 = mybir.dt.uint32
AF = mybir.ActivationFunctionType
ALU = mybir.AluOpType
AX = mybir.AxisListType

D = 1024
DEPTH = 2
A_W = 1536
B_W = 1024
IN_COLS = 8192
NEXP = 16384
RMS_EPS = 1e-6
LN_EPS = 1e-5
FILTER_EPS = 1e-6
PI = math.pi

ENG_BLK = {"sp": "sync", "pe": "tensor", "act": "scalar", "dve": "vector", "pool": "gpsimd"}


class Ctx:
    def __init__(self, nc, stack, n_dma_sems=48):
        self.nc = nc
        self.semh = {}
        self.semval = {}
        for e in ENG_BLK:
            if e == "sp":
                continue
            self.semh[e] = stack.enter_context(nc.semaphore("sem_" + e))
            self.semval[e] = 0
        self.free_dma = []
        for i in range(n_dma_sems):
            nm = "dq%d" % i
            self.semh[nm] = stack.enter_context(nc.semaphore(nm))
            self.semval[nm] = 0
            self.free_dma.append(nm)
        self.dma_key2sem = {}
        self.known = {e: {} for e in ENG_BLK}
        self.reset_phase()

    def reset_phase(self):
        self.ops = {e: [] for e in ENG_BLK}
        self.lastw = {}
        self.readers = {}
        for k, s in self.dma_key2sem.items():
            self.free_dma.append(s)
        self.dma_key2sem = {}

    def _deps(self, e, r, w):
        deps = {}

        def add(m):
            sk, v = m
            if deps.get(sk, 0) < v:
                deps[sk] = v

        for k in r:
            if k in self.lastw:
                add(self.lastw[k])
        for k in w:
            if k in self.lastw:
                add(self.lastw[k])
            for m in self.readers.get(k, ()):
                add(m)
        waits = []
        for sk, v in deps.items():
            if e == "pe" and sk == "pe":
                continue
            if self.known[e].get(sk, 0) >= v:
                continue
            waits.append((sk, v))
            self.known[e][sk] = v
        return waits

    def _commit(self, mark, r, w):
        for k in w:
            self.lastw[k] = mark
            self.readers[k] = []
        for k in r:
            if k not in w:
                self.readers.setdefault(k, []).append(mark)

    def defer_begin(self):
        self._defer = []

    def defer_end(self):
        d, self._defer = self._defer, None
        return d

    def replay(self, pending, n):
        while n > 0 and pending:
            kind, args = pending.pop(0)
            if kind == "op":
                self.op(*args)
            else:
                self.dma(*args)
            n -= 1

    def op(self, e, fn, r=(), w=()):
        if getattr(self, "_defer", None) is not None:
            self._defer.append(("op", (e, fn, tuple(r), tuple(w))))
            return
        r = tuple(r)
        w = tuple(w)
        waits = self._deps(e, r, w)
        self.semval[e] += 1
        mark = (e, self.semval[e])
        self.ops[e].append((waits, [fn], e, 1))
        self._commit(mark, r, w)

    def dma(self, e, fns, key, r=(), w=()):
        if getattr(self, "_defer", None) is not None:
            self._defer.append(("dma", (e, fns, key, tuple(r), tuple(w))))
            return
        if not isinstance(fns, (list, tuple)):
            fns = [fns]
        r = tuple(r)
        w = tuple(w)
        waits = self._deps(e, r, w)
        if key not in self.dma_key2sem:
            self.dma_key2sem[key] = self.free_dma.pop()
        sk = self.dma_key2sem[key]
        self.semval[sk] += 16 * len(fns)
        mark = (sk, self.semval[sk])
        self.ops[e].append((waits, list(fns), sk, 16))
        self._commit(mark, r, w)

    def flush(self):
        nc = self.nc
        finals = dict(self.semval)
        with nc.Block() as blk:
            for e, bname in ENG_BLK.items():
                ops = self.ops[e]
                known = self.known[e]

                def body(eng, ops=ops, known=known):
                    for waits, fns, sk, inc in ops:
                        for wk, v in waits:
                            eng.wait_ge(self.semh[wk], v)
                        for fn in fns:
                            fn(eng).then_inc(self.semh[sk], inc)
                    for sk, v in finals.items():
                        if v > 0 and known.get(sk, 0) < v:
                            eng.wait_ge(self.semh[sk], v)
                            known[sk] = v

                getattr(blk, bname)(body)
        self.reset_phase()


def dft_mats(L):
    t = np.arange(L, dtype=np.float64)[:, None]
    f = np.arange(L, dtype=np.float64)[None, :]
    ang = 2.0 * np.pi * (f + 0.5) * t / (2 * L)
    c = np.cos(ang)
    s = -np.sin(ang)
    nb = L // 128
    fwd = np.empty((L, 2 * L), np.float64)
    fv = fwd.reshape(L, nb, 2, 128)
    fv[:, :, 0, :] = c.reshape(L, nb, 128)
    fv[:, :, 1, :] = s.reshape(L, nb, 128)
    inv = fwd.T / L
    fwdP = fwd.reshape(nb, 128, nb, 256).transpose(2, 1, 0, 3)
    invP = inv.reshape(2 * nb, 128, nb, 128).transpose(2, 1, 0, 3)
    return (np.ascontiguousarray(fwdP).astype(ml_dtypes.bfloat16),
            np.ascontiguousarray(invP).astype(ml_dtypes.bfloat16))


def filter_feats(L):
    f32 = np.float32
    t = np.linspace(0.0, 1.0, L, dtype=f32)[:, None]
    bands = 16
    wpos = (2.0 * math.pi * np.arange(L, dtype=f32)[:, None] / L).astype(f32)
    freqs = np.linspace(1e-4, bands - 1, bands, dtype=f32)[None, :]
    feat = np.concatenate([t, np.cos(freqs * wpos), -np.sin(freqs * wpos)], axis=-1).astype(f32)
    featT = np.ascontiguousarray(feat.T)
    negt = np.ascontiguousarray((-t[:, 0]).reshape(L // 128, 128).T).astype(f32)
    return featT, negt


def abs_deltas():
    max_decay = math.log(1e-2) / 0.3
    min_decay = math.log(1e-2) / 1.5
    d = np.linspace(min_decay, max_decay, B_W, dtype=np.float32)
    return np.abs(d).reshape(1, B_W).astype(np.float32)


WEIGHT_SPECS = [
    ("norm_mix", [DEPTH, D]), ("w_in", [DEPTH, D, IN_COLS]), ("gm_ln_g", [DEPTH, A_W]), ("gm_ln_b", [DEPTH, A_W]),
    ("gm_ws", [DEPTH, 12, 128, 128]), ("gm_bs", [DEPTH, 12, 128]), ("hy_conv_w", [DEPTH, 3, 3 * B_W]),
    ("hy_conv_b", [DEPTH, 3 * B_W]), ("hy_f_w1", [DEPTH, 33, 64]), ("hy_f_b1", [DEPTH, 64]),
    ("hy_f_freq1", [DEPTH, 64]), ("hy_f_w2", [DEPTH, 64, 64]), ("hy_f_b2", [DEPTH, 64]), ("hy_f_freq2", [DEPTH, 64]),
    ("hy_f_w3", [DEPTH, 64, 64]), ("hy_f_b3", [DEPTH, 64]), ("hy_f_freq3", [DEPTH, 64]),
    ("hy_f_w4", [DEPTH, 64, 4 * B_W]), ("hy_skip", [DEPTH, 2, B_W]), ("w_a_out", [DEPTH, A_W, D]),
    ("w_b_out", [DEPTH, B_W, D]), ("w_o", [DEPTH, D, D]), ("norm_ffn", [DEPTH, D]), ("peer_wq", [DEPTH, D, 2048]),
    ("peer_keys", [DEPTH, 8, 2, 128, 128]), ("peer_u", [DEPTH, NEXP, D]), ("peer_v", [DEPTH, NEXP, D]),
    ("final_norm", [D]),
]


class Prog:
    def __init__(self, seqs, dbg=None, phases=None):
        self.seqs = list(seqs)
        self.T = sum(seqs)
        self.NT = self.T // 128
        self.Ls = sorted(set(seqs))
        self.dbg = dbg or ()
        self.phases = phases
        self.nc = bass.Bass("TRN2", target_bir_lowering=False)
        nc = self.nc
        self.inp = {}
        self.inp["x"] = nc.dram_tensor("x", [self.T, D], F32, kind="ExternalInput").ap()
        for nm, shp in WEIGHT_SPECS:
            self.inp[nm] = nc.dram_tensor(nm, shp, F32, kind="ExternalInput").ap()
        self.inp["ident_b"] = nc.dram_tensor("ident_b", [128, 128], BF16, kind="ExternalInput").ap()
        self.inp["ident_f"] = nc.dram_tensor("ident_f", [128, 128], F32, kind="ExternalInput").ap()
        self.inp["absdelta"] = nc.dram_tensor("absdelta", [1, B_W], F32, kind="ExternalInput").ap()
        self.inp["iota16"] = nc.dram_tensor("iota16", [128, 16], F32, kind="ExternalInput").ap()
        for L in self.Ls:
            nb = L // 128
            self.inp["fwd%d" % L] = nc.dram_tensor("fwd%d" % L, [nb, 128, nb, 256], BF16, kind="ExternalInput").ap()
            self.inp["inv%d" % L] = nc.dram_tensor("inv%d" % L, [nb, 128, 2 * nb, 128], BF16, kind="ExternalInput").ap()
            self.inp["featT%d" % L] = nc.dram_tensor("featT%d" % L, [33, L], F32, kind="ExternalInput").ap()
            self.inp["negt%d" % L] = nc.dram_tensor("negt%d" % L, [128, L // 128], F32, kind="ExternalInput").ap()
        self.y = nc.dram_tensor("y", [self.T, D], F32, kind="ExternalOutput").ap()

        def scratch(name, shape, dt):
            kind = "ExternalOutput" if name in self.dbg else "Internal"
            return nc.dram_tensor(name, shape, dt, kind=kind).ap()

        T = self.T
        self.s_win = scratch("s_win", [DEPTH, D, IN_COLS], BF16)
        self.s_wao = scratch("s_wao", [DEPTH, A_W, D], BF16)
        self.s_wbo = scratch("s_wbo", [DEPTH, B_W, D], BF16)
        self.s_wo = scratch("s_wo", [DEPTH, D, D], BF16)
        self.s_wq = scratch("s_wq", [DEPTH, D, 2048], BF16)
        self.s_uv = [scratch("s_uv%d" % l, [NEXP, 2, D], BF16) for l in range(DEPTH)]
        self.s_xres = scratch("s_xres", [T, D], F32)
        self.s_zb = scratch("s_zb", [T, 3 * B_W], BF16)
        self.s_xc = scratch("s_xc", [T, 2, B_W], BF16)
        self.s_gaya = scratch("s_gaya", [T, D], BF16)
        self.s_gb = scratch("s_gb", [T, D], BF16)
        self.s_yb = scratch("s_yb", [T, D], BF16)
        self.s_kf = {L: scratch("s_kf%d" % L, [2 * L, 2, B_W], BF16) for L in self.Ls}

    def build(self):
        nc = self.nc
        with ExitStack() as stack:
            self.cx = Ctx(nc, stack)
            self.ps = stack.enter_context(nc.psum_tensor("ps", [128, 8, 512], F32))
            self.identb = stack.enter_context(nc.sbuf_tensor("identb", [128, 128], BF16))
            self.identf = stack.enter_context(nc.sbuf_tensor("identf", [128, 128], F32))
            cx = self.cx
            cx.dma("sp", lambda e: e.dma_start(out=self.identb[:], in_=self.inp["ident_b"][:, :]), "c0", w=["identb"])
            cx.dma("sp", lambda e: e.dma_start(out=self.identf[:], in_=self.inp["ident_f"][:, :]), "c1", w=["identf"])
            cx.flush()
            ph = self.phases
            if ph is None or "W" in ph:
                self.phase_W()
            for l in range(DEPTH):
                if ph is None or "A1" in ph:
                    self.phase_A1(l)
                if ph is None or "A2" in ph:
                    self.phase_A2(l)
                if ph is None or "F" in ph:
                    for L in self.Ls:
                        self.phase_F(l, L)
                if ph is None or "B" in ph:
                    self.phase_B(l)
                if ph is None or "C" in ph or "C1" in ph:
                    self.phase_C1(l)
                if ph is None or "C" in ph or "C2" in ph:
                    self.phase_C2(l)
                if ph is not None and "L0" in ph:
                    break
        return nc

    def uname(self, name):
        self._uid = getattr(self, "_uid", 0) + 1
        return "%s_%d" % (name, self._uid)

    def psb(self, b):
        return self.ps[:, b, :].bitcast(BF16)

    def phase_W(self):
        nc, cx = self.nc, self.cx
        jobs = [(self.inp["w_in"], self.s_win), (self.inp["w_a_out"], self.s_wao), (self.inp["w_b_out"], self.s_wbo),
                (self.inp["w_o"], self.s_wo), (self.inp["peer_wq"], self.s_wq)]
        CH = 4096
        NS = 4
        with ExitStack() as st:
            fb = [st.enter_context(nc.sbuf_tensor(self.uname("wf%d" % i), [128, CH], F32)) for i in range(NS)]
            bb = [st.enter_context(nc.sbuf_tensor(self.uname("wb%d" % i), [128, CH], BF16)) for i in range(NS)]
            it = 0
            engs = ["dve", "pool"]
            work = []
            for src, dst in jobs:
                n = int(np.prod(src.shape)) // 128
                srcf = _flat128(src)
                dstf = _flat128(dst)
                for c0 in range(0, n, CH):
                    cw = min(CH, n - c0)
                    work.append((srcf[:, c0:c0 + cw], dstf[:, c0:c0 + cw], cw, None))
            for l in range(DEPTH):
                for which, nm in enumerate(("peer_u", "peer_v")):
                    srcv = self.inp[nm][l].rearrange("(p r) d -> p r d", p=128)
                    dstv = self.s_uv[l].rearrange("(p r) two d -> p r two d", p=128)[:, :, which, :]
                    for r0 in range(0, 128, 4):
                        work.append((srcv[:, r0:r0 + 4, :], dstv[:, r0:r0 + 4, :], CH, 4))
            for srca, dsta, cw, rows in work:
                s_ = it % NS
                fview = fb[s_][:, 0:cw] if rows is None else fb[s_][:, 0:cw].rearrange("p (r d) -> p r d", r=rows)
                bview = bb[s_][:, 0:cw] if rows is None else bb[s_][:, 0:cw].rearrange("p (r d) -> p r d", r=rows)
                cx.dma("sp", lambda e, fview=fview, srca=srca: e.dma_start(out=fview, in_=srca), ("wf", s_), w=[("wf", s_)])
                eng = engs[it % 2]
                cx.op(eng, lambda e, s_=s_, cw=cw: e.tensor_copy(out=bb[s_][:, 0:cw], in_=fb[s_][:, 0:cw]), r=[("wf", s_)], w=[("wb", s_)])
                cx.dma("act", lambda e, bview=bview, dsta=dsta: e.dma_start(out=dsta, in_=bview), ("wb", s_), r=[("wb", s_)])
                it += 1
            cx.flush()

    def rms_to_hT(self, xt, xkey, gt, tl, bank, hn_out_key="hn", hT_key="hT"):
        cx = self.cx
        junk, ss, rstd, hn, hT = tl["junk"], tl["ss"], tl["rstd"], tl["hn"], tl["hT"]
        cx.op("act", lambda e: e.activation(out=junk[:], in_=xt[:], func=AF.Square, accum_out=ss[:, 0:1]),
              r=[xkey], w=["junk", "ss"])
        cx.op("dve", lambda e: e.tensor_scalar(out=rstd[:], in0=ss[:], scalar1=1.0 / D, scalar2=RMS_EPS,
                                               op0=ALU.mult, op1=ALU.add), r=["ss"], w=["rstd"])
        cx.op("act", lambda e: e.sqrt(out=rstd[:], in_=rstd[:]), r=["rstd"], w=["rstd"])
        cx.op("dve", lambda e: e.reciprocal(out=rstd[:], in_=rstd[:]), r=["rstd"], w=["rstd"])
        cx.op("dve", lambda e: e.scalar_tensor_tensor(out=hn[:], in0=xt[:], scalar=rstd[:, 0:1], in1=gt[:],
                                                      op0=ALU.mult, op1=ALU.mult), r=[xkey, "rstd", "g"], w=[hn_out_key])
        pv = self.psb(bank)
        for k in range(8):
            cx.op("pe", lambda e, k=k: e.transpose(out=pv[:, k * 128:(k + 1) * 128], in_=hn[:, k * 128:(k + 1) * 128],
                                                   identity=self.identb[:]), r=[hn_out_key, "identb"], w=[("ps", bank)])
        cx.op("act", lambda e: e.copy(out=hT[:].rearrange("p k t -> p (k t)"), in_=pv[:, :]), r=[("ps", bank)], w=[hT_key])

    def phase_A1(self, l):
        nc, cx, ps = self.nc, self.cx, self.ps
        with ExitStack() as st:
            def sb(name, shape, dt):
                return st.enter_context(nc.sbuf_tensor(self.uname(name), shape, dt))
            win = sb("a1_win", [128, 8, 4096], BF16)
            wao = sb("a1_wao", [128, 12, 1024], BF16)
            wsn = sb("a1_wsn", [128, 12, 128], F32)
            wsb = sb("a1_wsb", [128, 12, 128], BF16)
            wsT = sb("a1_wsT", [128, 12, 128], BF16)
            bsT = sb("a1_bsT", [128, 12], F32)
            gmix = sb("a1_g", [128, D], F32)
            lng = sb("a1_lng", [128, A_W], F32)
            lnb = sb("a1_lnb", [128, A_W], F32)
            xts = [sb("a1_x%d" % i, [128, D], F32) for i in range(2)]
            tl = dict(junk=sb("a1_junk", [128, D], BF16), ss=sb("a1_ss", [128, 1], F32), rstd=sb("a1_rstd", [128, 1], F32),
                      hn=sb("a1_hn", [128, D], BF16), hT=sb("a1_hT", [128, 8, 128], BF16))
            a_s = [sb("a1_a%d" % i, [128, 2 * A_W], F32) for i in range(2)]
            ga_s = [sb("a1_ga%d" % i, [128, D], BF16) for i in range(2)]
            hns = [sb("a1_hn%d" % i, [128, D], BF16) for i in range(2)]
            hTs = [sb("a1_hT%d" % i, [128, 8, 128], BF16) for i in range(2)]
            stats = sb("a1_stats", [128, 3, 6], F32)
            mv = sb("a1_mv", [128, 2], F32)
            lrs = sb("a1_lrs", [128, 1], F32)
            vn32 = sb("a1_vn32", [128, A_W], F32)
            vn = sb("a1_vn", [128, A_W], BF16)
            gA = sb("a1_gA", [128, A_W], BF16)
            gAT = sb("a1_gAT", [128, 12, 128], BF16)
            outs = [sb("a1_o%d" % i, [128, D], BF16) for i in range(2)]

            wl = self.s_win[l]
            cx.dma("sp", [lambda e, k=k: e.dma_start(out=win[:, k, 0:3072], in_=wl[k * 128:(k + 1) * 128, 0:3072]) for k in range(8)]
                   + [lambda e, k=k: e.dma_start(out=win[:, k, 3072:4096], in_=wl[k * 128:(k + 1) * 128, 6144:7168]) for k in range(8)],
                   "win", w=["win"])
            cx.dma("sp", lambda e: e.dma_start(out=wao[:], in_=self.s_wao[l].rearrange("(j p) n -> p j n", p=128)), "wao", w=["wao"])
            cx.dma("sp", lambda e: e.dma_start(out=wsn[:], in_=self.inp["gm_ws"][l].rearrange("g q p -> q g p")), "wsn", w=["wsn"])
            cx.dma("sp", lambda e: e.dma_start(out=bsT[:], in_=self.inp["gm_bs"][l].rearrange("g q -> q g"),
                                               allow_slow_non_contiguous=True), "bsT", w=["bsT"])
            cx.dma("sp", lambda e: e.dma_start(out=gmix[:], in_=self.inp["norm_mix"][l:l + 1, :].partition_broadcast(128)), "g", w=["g"])
            cx.dma("sp", lambda e: e.dma_start(out=lng[:], in_=self.inp["gm_ln_g"][l:l + 1, :].partition_broadcast(128)), "lng", w=["lng"])
            cx.dma("sp", lambda e: e.dma_start(out=lnb[:], in_=self.inp["gm_ln_b"][l:l + 1, :].partition_broadcast(128)), "lnb", w=["lnb"])
            cx.op("dve", lambda e: e.tensor_copy(out=wsb[:], in_=wsn[:]), r=["wsn"], w=["wsb"])
            for half in range(2):
                bank = 6 + half
                pv = self.psb(bank)
                gs = list(range(half * 8, min(12, half * 8 + 8)))
                for j, g in enumerate(gs):
                    cx.op("pe", lambda e, j=j, g=g, pv=pv: e.transpose(out=pv[:, j * 128:(j + 1) * 128], in_=wsb[:, g, :],
                                                                      identity=self.identb[:]), r=["wsb"], w=[("ps", bank)])
                n = len(gs)
                cx.op("act", lambda e, pv=pv, n=n, g0=gs[0]: e.copy(
                    out=wsT[:, g0:g0 + n, :].rearrange("p g q -> p (g q)"), in_=pv[:, 0:n * 128]), r=[("ps", bank)], w=["wsT"])

            src = self.inp["x"] if l == 0 else self.s_xres

            def S1(i):
                p = i % 2
                xt = xts[p]
                r0 = i * 128
                a, ga = a_s[p], ga_s[p]
                cx.dma("sp", lambda e: e.dma_start(out=xt[:], in_=src[r0:r0 + 128, :]), ("x", p), w=[("x", p)])
                tlp = dict(tl); tlp["hn"] = hns[p]; tlp["hT"] = hTs[p]
                self.rms_to_hT(xt, ("x", p), gmix, tlp, bank=2, hn_out_key=("hn", p), hT_key=("hT", p))
                hT = hTs[p]
                for nb in range(8):
                    bank = nb % 2
                    for k in range(8):
                        cx.op("pe", lambda e, nb=nb, k=k, bank=bank: e.matmul(
                            ps[:, bank, :], lhsT=hT[:, k, :], rhs=win[:, k, nb * 512:(nb + 1) * 512], start=(k == 0), stop=(k == 7)),
                            r=[("hT", p), "win"], w=[("ps", bank)])
                    if nb < 6:
                        cx.op("act", lambda e, nb=nb, bank=bank: e.activation(out=a[:, nb * 512:(nb + 1) * 512], in_=ps[:, bank, :], func=AF.Gelu),
                              r=[("ps", bank)], w=[("a", nb, p)])
                    else:
                        cx.op("act", lambda e, nb=nb, bank=bank: e.activation(out=ga[:, (nb - 6) * 512:(nb - 5) * 512], in_=ps[:, bank, :],
                                                                             func=AF.Sigmoid), r=[("ps", bank)], w=[("ga", p)])

            def S2(i, pend):
                p = i % 2
                r0 = i * 128
                a, ga = a_s[p], ga_s[p]
                n1 = 4 * 9
                cx.replay(pend, 15)
                for c in range(3):
                    cx.op("dve", lambda e, c=c: e.bn_stats(out=stats[:, c, :], in_=a[:, A_W + c * 512:A_W + (c + 1) * 512]),
                          r=[("a", 3 + c, p)], w=["stats"])
                cx.op("dve", lambda e: e.bn_aggr(out=mv[:], in_=stats[:].rearrange("p c s -> p (c s)")), r=["stats"], w=["mv"])
                cx.op("dve", lambda e: e.tensor_scalar_add(out=lrs[:], in0=mv[:, 1:2], scalar1=LN_EPS), r=["mv"], w=["lrs"])
                cx.op("act", lambda e: e.sqrt(out=lrs[:], in_=lrs[:]), r=["lrs"], w=["lrs"])
                cx.op("dve", lambda e: e.reciprocal(out=lrs[:], in_=lrs[:]), r=["lrs"], w=["lrs"])
                cx.op("dve", lambda e: e.tensor_scalar(out=vn32[:], in0=a[:, A_W:2 * A_W], scalar1=mv[:, 0:1], scalar2=lrs[:, 0:1],
                                                       op0=ALU.subtract, op1=ALU.mult), r=[("a", 3, p), ("a", 4, p), ("a", 5, p), "mv", "lrs"], w=["vn32"])
                cx.op("pool", lambda e: e.tensor_tensor(out=vn32[:], in0=vn32[:], in1=lng[:], op=ALU.mult), r=["vn32", "lng"], w=["vn32"])
                cx.op("dve", lambda e: e.tensor_tensor(out=vn[:], in0=vn32[:], in1=lnb[:], op=ALU.add), r=["vn32", "lnb"], w=["vn"])
                cx.replay(pend, n1)
                for g in range(12):
                    bank = 3 + g // 4
                    cx.op("pe", lambda e, g=g, bank=bank: e.matmul(ps[:, bank, (g % 4) * 128:(g % 4 + 1) * 128], lhsT=wsT[:, g, :],
                                                                  rhs=vn[:, g * 128:(g + 1) * 128], start=True, stop=True),
                          r=["wsT", "vn"], w=[("ps", bank)])
                cx.replay(pend, 18)
                for g in range(12):
                    bank = 3 + g // 4
                    cx.op("dve", lambda e, g=g, bank=bank: e.scalar_tensor_tensor(
                        out=gA[:, g * 128:(g + 1) * 128], in0=ps[:, bank, (g % 4) * 128:(g % 4 + 1) * 128], scalar=bsT[:, g:g + 1],
                        in1=a[:, g * 128:(g + 1) * 128], op0=ALU.add, op1=ALU.mult),
                        r=[("ps", bank), "bsT", ("a", g // 4, p)], w=["gA"])
                for half in range(2):
                    bank = 6 + half
                    pv = self.psb(bank)
                    js = list(range(half * 8, min(12, half * 8 + 8)))
                    for jj, j in enumerate(js):
                        cx.op("pe", lambda e, jj=jj, j=j, pv=pv: e.transpose(out=pv[:, jj * 128:(jj + 1) * 128], in_=gA[:, j * 128:(j + 1) * 128],
                                                                            identity=self.identb[:]), r=["gA"], w=[("ps", bank)])
                    n = len(js)
                    cx.op("act", lambda e, pv=pv, n=n, j0=js[0]: e.copy(out=gAT[:, j0:j0 + n, :].rearrange("p j t -> p (j t)"),
                                                                       in_=pv[:, 0:n * 128]), r=[("ps", bank)], w=["gAT"])
                cx.replay(pend, len(pend))
                ot = outs[p]
                for nb in range(2):
                    bank = 3 + nb
                    for j in range(12):
                        cx.op("pe", lambda e, nb=nb, j=j, bank=bank: e.matmul(ps[:, bank, :], lhsT=gAT[:, j, :],
                                                                             rhs=wao[:, j, nb * 512:(nb + 1) * 512], start=(j == 0), stop=(j == 11)),
                              r=["gAT", "wao"], w=[("ps", bank)])
                    cx.op("dve", lambda e, nb=nb, bank=bank: e.tensor_tensor(out=ot[:, nb * 512:(nb + 1) * 512], in0=ps[:, bank, :],
                                                                            in1=ga[:, nb * 512:(nb + 1) * 512], op=ALU.mult),
                          r=[("ps", bank), ("ga", p)], w=[("o", p)])
                cx.dma("pool", lambda e: e.dma_start(out=self.s_gaya[r0:r0 + 128, :], in_=ot[:]), ("o", p), r=[("o", p)])

            S1(0)
            for i in range(self.NT):
                pend = []
                if i + 1 < self.NT:
                    cx.defer_begin()
                    S1(i + 1)
                    pend = cx.defer_end()
                S2(i, pend)
            cx.flush()

    def phase_A2(self, l):
        nc, cx, ps = self.nc, self.cx, self.ps
        with ExitStack() as st:
            def sb(name, shape, dt):
                return st.enter_context(nc.sbuf_tensor(self.uname(name), shape, dt))
            win = sb("a2_win", [128, 8, 4096], BF16)
            gmix = sb("a2_g", [128, D], F32)
            xts = [sb("a2_x%d" % i, [128, D], F32) for i in range(2)]
            tl = dict(junk=sb("a2_junk", [128, D], BF16), ss=sb("a2_ss", [128, 1], F32), rstd=sb("a2_rstd", [128, 1], F32),
                      hn=sb("a2_hn", [128, D], BF16), hT=sb("a2_hT", [128, 8, 128], BF16))
            hns = [sb("a2_hn%d" % i, [128, D], BF16) for i in range(2)]
            hTs = [sb("a2_hT%d" % i, [128, 8, 128], BF16) for i in range(2)]
            zbs = [sb("a2_zb%d" % i, [128, 3 * B_W], BF16) for i in range(2)]
            gbs = [sb("a2_gb%d" % i, [128, D], BF16) for i in range(2)]
            wl = self.s_win[l]
            cx.dma("sp", [lambda e, k=k: e.dma_start(out=win[:, k, 0:3072], in_=wl[k * 128:(k + 1) * 128, 3072:6144]) for k in range(8)]
                   + [lambda e, k=k: e.dma_start(out=win[:, k, 3072:4096], in_=wl[k * 128:(k + 1) * 128, 7168:8192]) for k in range(8)],
                   "win", w=["win"])
            cx.dma("sp", lambda e: e.dma_start(out=gmix[:], in_=self.inp["norm_mix"][l:l + 1, :].partition_broadcast(128)), "g", w=["g"])
            src = self.inp["x"] if l == 0 else self.s_xres

            def Sa(i):
                p = i % 2
                xt = xts[p]
                r0 = i * 128
                cx.dma("sp", lambda e: e.dma_start(out=xt[:], in_=src[r0:r0 + 128, :]), ("x", p), w=[("x", p)])
                tlp = dict(tl); tlp["hn"] = hns[p]; tlp["hT"] = hTs[p]
                self.rms_to_hT(xt, ("x", p), gmix, tlp, bank=2 + p, hn_out_key=("hn", p), hT_key=("hT", p))

            def Sb(i, pend):
                p = i % 2
                r0 = i * 128
                hT = hTs[p]
                zb, gb = zbs[p], gbs[p]
                cx.replay(pend, 6)
                for nb in range(8):
                    bank = nb % 2
                    for k in range(8):
                        cx.op("pe", lambda e, nb=nb, k=k, bank=bank: e.matmul(
                            ps[:, bank, :], lhsT=hT[:, k, :], rhs=win[:, k, nb * 512:(nb + 1) * 512], start=(k == 0), stop=(k == 7)),
                            r=[("hT", p), "win"], w=[("ps", bank)])
                    if nb < 6:
                        if nb % 2 == 0:
                            cx.op("dve", lambda e, nb=nb, bank=bank: e.tensor_copy(out=zb[:, nb * 512:(nb + 1) * 512], in_=ps[:, bank, :]),
                                  r=[("ps", bank)], w=[("zb", p)])
                        else:
                            cx.op("act", lambda e, nb=nb, bank=bank: e.copy(out=zb[:, nb * 512:(nb + 1) * 512], in_=ps[:, bank, :]),
                                  r=[("ps", bank)], w=[("zb", p)])
                    else:
                        cx.op("act", lambda e, nb=nb, bank=bank: e.activation(out=gb[:, (nb - 6) * 512:(nb - 5) * 512], in_=ps[:, bank, :],
                                                                             func=AF.Sigmoid), r=[("ps", bank)], w=[("gb", p)])
                    if nb == 3:
                        cx.replay(pend, len(pend))
                cx.dma("pool", lambda e: e.dma_start(out=self.s_zb[r0:r0 + 128, :], in_=zb[:]), ("zb", p), r=[("zb", p)])
                cx.dma("pool", lambda e: e.dma_start(out=self.s_gb[r0:r0 + 128, :], in_=gb[:]), ("gb", p), r=[("gb", p)])

            Sa(0)
            for i in range(self.NT):
                pend = []
                if i + 1 < self.NT:
                    cx.defer_begin()
                    Sa(i + 1)
                    pend = cx.defer_end()
                Sb(i, pend)
            cx.flush()

    def phase_F(self, l, L):
        nc, cx, ps = self.nc, self.cx, self.ps
        NTs = L // 128
        inp = self.inp
        with ExitStack() as st_outer:
          hA = st_outer.enter_context(nc.sbuf_tensor(self.uname("f_hA"), [64, L], F32))
          h3b = st_outer.enter_context(nc.sbuf_tensor(self.uname("f_h3b"), [64, L], BF16))
          with ExitStack() as st:
            def sb(name, shape, dt):
                return st.enter_context(nc.sbuf_tensor(self.uname(name), shape, dt))
            featT = sb("f_feat", [33, L], F32)
            w1 = sb("f_w1", [33, 64], F32)
            w2 = sb("f_w2", [64, 64], F32)
            w3 = sb("f_w3", [64, 64], F32)
            bfr = sb("f_bfr", [64, 6], F32)
            hB = sb("f_hB", [64, L], F32)
            arg = sb("f_arg", [64, 512], F32)
            msk = sb("f_msk", [64, 512], F32)
            cx.dma("sp", lambda e: e.dma_start(out=featT[:], in_=inp["featT%d" % L][:, :]), "feat", w=["feat"])
            cx.dma("sp", lambda e: e.dma_start(out=w1[:], in_=inp["hy_f_w1"][l]), "w1", w=["w1"])
            cx.dma("sp", lambda e: e.dma_start(out=w2[:], in_=inp["hy_f_w2"][l]), "w2", w=["w2"])
            cx.dma("sp", lambda e: e.dma_start(out=w3[:], in_=inp["hy_f_w3"][l]), "w3", w=["w3"])
            names = ["hy_f_b1", "hy_f_b2", "hy_f_b3", "hy_f_freq1", "hy_f_freq2", "hy_f_freq3"]
            cx.dma("sp", [lambda e, i=i, nm=nm: e.dma_start(out=bfr[:, i:i + 1], in_=inp[nm][l].rearrange("(h o) -> h o", o=1))
                          for i, nm in enumerate(names)], "bfr", w=["bfr"])

            chain = [(featT, 33, w1, hA, "feat", "w1", "hA"), (hA, 64, w2, hB, "hA", "w2", "hB"), (hB, 64, w3, h3b, "hB", "w3", "hA2")]
            nblk = (L + 511) // 512
            for li, (src, kk, wt, dst, sk_, wk_, dk_) in enumerate(chain):
                for b in range(nblk):
                    c0 = b * 512
                    cw = min(512, L - c0)
                    bank = b % 2
                    rk = sk_ if li != 2 else "hB"
                    cx.op("pe", lambda e, src=src, kk=kk, wt=wt, c0=c0, cw=cw, bank=bank: e.matmul(
                        ps[0:64, bank, 0:cw], lhsT=wt[0:kk, :], rhs=src[0:kk, c0:c0 + cw], start=True, stop=True),
                        r=[sk_ if li == 0 else (sk_ if li == 1 else "hB"), wk_], w=[("ps", bank)])
                    cx.op("dve", lambda e, li=li, cw=cw, bank=bank: e.tensor_scalar(
                        out=arg[:, 0:cw], in0=ps[0:64, bank, 0:cw], scalar1=bfr[:, li:li + 1], scalar2=bfr[:, 3 + li:4 + li],
                        op0=ALU.add, op1=ALU.mult), r=[("ps", bank), "bfr"], w=["arg"])
                    cx.op("dve", lambda e, cw=cw: e.tensor_scalar(out=msk[:, 0:cw], in0=arg[:, 0:cw], scalar1=PI, scalar2=-2.0 * PI,
                                                                 op0=ALU.is_gt, op1=ALU.mult), r=["arg"], w=["msk"])
                    cx.op("dve", lambda e, cw=cw: e.tensor_tensor(out=arg[:, 0:cw], in0=arg[:, 0:cw], in1=msk[:, 0:cw], op=ALU.add),
                          r=["arg", "msk"], w=["arg"])
                    cx.op("dve", lambda e, cw=cw: e.tensor_scalar(out=msk[:, 0:cw], in0=arg[:, 0:cw], scalar1=-PI, scalar2=2.0 * PI,
                                                                 op0=ALU.is_lt, op1=ALU.mult), r=["arg"], w=["msk"])
                    cx.op("dve", lambda e, cw=cw: e.tensor_tensor(out=arg[:, 0:cw], in0=arg[:, 0:cw], in1=msk[:, 0:cw], op=ALU.add),
                          r=["arg", "msk"], w=["arg"])
                    wkey = dk_ if li != 2 else "h3b"
                    cx.op("act", lambda e, dst=dst, c0=c0, cw=cw: e.activation(out=dst[:, c0:c0 + cw], in_=arg[:, 0:cw], func=AF.Sin),
                          r=["arg"], w=[wkey])
            cx.flush()
          with ExitStack() as st:
            def sb(name, shape, dt):
                return st.enter_context(nc.sbuf_tensor(self.uname(name), shape, dt))
            w4f = sb("f_w4f", [64, 4 * B_W], F32)
            w4 = sb("f_w4", [64, 4 * B_W], BF16)
            absd = sb("f_absd", [128, B_W], F32)
            negt = sb("f_negt", [128, NTs], F32)
            ones = sb("f_ones", [128, 128], BF16)
            hf = sb("f_hf", [128, NTs, 512], BF16)
            hb = sb("f_hb", [128, NTs, 512], BF16)
            hp = sb("f_hp", [128, NTs, 512], BF16)
            win = [sb("f_win%d" % i, [128, 512], F32) for i in range(2)]
            absf = [sb("f_absf%d" % i, [128, 512], BF16) for i in range(2)]
            absb = [sb("f_absb%d" % i, [128, 512], BF16) for i in range(2)]
            sf = sb("f_sf", [128, 512], F32)
            sbk = sb("f_sb", [128, 512], F32)
            Fb = [sb("f_F%d" % i, [128, NTs, 256], BF16) for i in range(2)]
            ko = [sb("f_ko%d" % i, [128, 2, 512], BF16) for i in range(2)]
            cx.dma("sp", lambda e: e.dma_start(out=w4f[:], in_=inp["hy_f_w4"][l]), "w4", w=["w4f"])
            cx.op("dve", lambda e: e.tensor_copy(out=w4[:], in_=w4f[:]), r=["w4f"], w=["w4"])
            cx.dma("sp", lambda e: e.dma_start(out=absd[:], in_=inp["absdelta"][0:1, :].partition_broadcast(128)), "absd", w=["absd"])
            cx.dma("sp", lambda e: e.dma_start(out=negt[:], in_=inp["negt%d" % L][:, :]), "negt", w=["negt"])
            cx.op("dve", lambda e: e.memset(ones[:], 1.0), w=["ones"])
            h3 = h3b
            kf = self.s_kf[L]
            fwdP = inp["fwd%d" % L]
            it = 0
            for o in range(2):
                for cb in range(2):
                    colf = o * 2048 + cb * 512
                    colb = o * 2048 + 1024 + cb * 512
                    for j in range(NTs):
                        q = j % 2
                        bf_, bb_ = (0, 1) if j % 2 == 0 else (4, 5)
                        cx.op("pe", lambda e, j=j, colf=colf, bf_=bf_: e.matmul(ps[:, bf_, :], lhsT=h3[:, j * 128:(j + 1) * 128],
                                                                               rhs=w4[:, colf:colf + 512], start=True, stop=True),
                              r=["h3b", "w4"], w=[("ps", bf_)])
                        cx.op("pe", lambda e, j=j, colb=colb, bb_=bb_: e.matmul(ps[:, bb_, :], lhsT=h3[:, j * 128:(j + 1) * 128],
                                                                               rhs=w4[:, colb:colb + 512], start=True, stop=True),
                              r=["h3b", "w4"], w=[("ps", bb_)])
                        cx.op("act", lambda e, j=j, cb=cb, q=q: e.activation(out=win[q][:], in_=absd[:, cb * 512:(cb + 1) * 512], func=AF.Exp,
                                                                            scale=negt[:, j:j + 1]), r=["absd", "negt"], w=[("win", q)])
                        cx.op("dve", lambda e, j=j, q=q, bf_=bf_: e.tensor_tensor(out=hf[:, j, :], in0=ps[:, bf_, :], in1=win[q][:], op=ALU.mult),
                              r=[("ps", bf_), ("win", q)], w=[("hf", j)])
                        cx.op("dve", lambda e, j=j, q=q, bb_=bb_: e.tensor_tensor(out=hb[:, j, :], in0=ps[:, bb_, :], in1=win[q][:], op=ALU.mult),
                              r=[("ps", bb_), ("win", q)], w=[("hb", j)])
                        if j == 0:
                            cx.op("dve", lambda e: e.memset(hb[0:1, 0, :], 0.0), w=[("hb", 0)])
                        cx.op("act", lambda e, j=j, q=q: e.activation(out=absf[q][:], in_=hf[:, j, :], func=AF.Abs),
                              r=[("hf", j)], w=[("absf", q)])
                        cx.op("act", lambda e, j=j, q=q: e.activation(out=absb[q][:], in_=hb[:, j, :], func=AF.Abs),
                              r=[("hb", j)], w=[("absb", q)])
                        cx.op("pe", lambda e, j=j, q=q: e.matmul(ps[:, 2, :], lhsT=ones[:], rhs=absf[q][:], start=(j == 0), stop=(j == NTs - 1)),
                              r=["ones", ("absf", q)], w=[("ps", 2)])
                        cx.op("pe", lambda e, j=j, q=q: e.matmul(ps[:, 3, :], lhsT=ones[:], rhs=absb[q][:], start=(j == 0), stop=(j == NTs - 1)),
                              r=["ones", ("absb", q)], w=[("ps", 3)])
                    cx.op("dve", lambda e: e.tensor_scalar_add(out=sf[:], in0=ps[:, 2, :], scalar1=FILTER_EPS), r=[("ps", 2)], w=["sf"])
                    cx.op("dve", lambda e: e.reciprocal(out=sf[:], in_=sf[:]), r=["sf"], w=["sf"])
                    cx.op("dve", lambda e: e.tensor_scalar_add(out=sbk[:], in0=ps[:, 3, :], scalar1=FILTER_EPS), r=[("ps", 3)], w=["sb"])
                    cx.op("dve", lambda e: e.reciprocal(out=sbk[:], in_=sbk[:]), r=["sb"], w=["sb"])
                    allf = [("hf", j) for j in range(NTs)]
                    allb = [("hb", j) for j in range(NTs)]
                    allp = [("hp", j) for j in range(NTs)]
                    J1 = max(1, (3 * NTs) // 4)
                    for (eng, ja, jb) in (("dve", 0, J1), ("pool", J1, NTs)):
                        if jb <= ja:
                            continue
                        nj = jb - ja
                        kf_ = [("hf", j) for j in range(ja, jb)]
                        kb_ = [("hb", j) for j in range(ja, jb)]
                        kp_ = [("hp", j) for j in range(ja, jb)]
                        cx.op(eng, lambda e, ja=ja, jb=jb, nj=nj: e.tensor_tensor(out=hf[:, ja:jb, :], in0=hf[:, ja:jb, :],
                                                                                 in1=sf[:].unsqueeze(1).to_broadcast([128, nj, 512]), op=ALU.mult),
                              r=kf_ + ["sf"], w=kf_)
                        cx.op(eng, lambda e, ja=ja, jb=jb, nj=nj: e.tensor_tensor(out=hb[:, ja:jb, :], in0=hb[:, ja:jb, :],
                                                                                 in1=sbk[:].unsqueeze(1).to_broadcast([128, nj, 512]), op=ALU.mult),
                              r=kb_ + ["sb"], w=kb_)
                        cx.op(eng, lambda e, ja=ja, jb=jb: e.tensor_tensor(out=hp[:, ja:jb, :], in0=hf[:, ja:jb, :], in1=hb[:, ja:jb, :], op=ALU.add),
                              r=kf_ + kb_, w=kp_)
                        cx.op(eng, lambda e, ja=ja, jb=jb: e.tensor_tensor(out=hf[:, ja:jb, :], in0=hf[:, ja:jb, :], in1=hb[:, ja:jb, :], op=ALU.subtract),
                              r=kf_ + kb_, w=kf_)
                    for i in range(NTs):
                        q = it % 2
                        it += 1
                        cx.dma("sp", lambda e, i=i, q=q: e.dma_start(out=Fb[q][:], in_=fwdP[i]), ("F", q), w=[("F", q)])
                        bre, bim = 4 + 2 * q, 5 + 2 * q
                        for j in range(NTs):
                            cx.op("pe", lambda e, j=j, q=q, bre=bre: e.matmul(ps[:, bre, :], lhsT=Fb[q][:, j, 0:128], rhs=hp[:, j, :],
                                                                             start=(j == 0), stop=(j == NTs - 1)),
                                  r=[("F", q), ("hp", j)], w=[("ps", bre)])
                            cx.op("pe", lambda e, j=j, q=q, bim=bim: e.matmul(ps[:, bim, :], lhsT=Fb[q][:, j, 128:256], rhs=hf[:, j, :],
                                                                             start=(j == 0), stop=(j == NTs - 1)),
                                  r=[("F", q), ("hf", j)], w=[("ps", bim)])
                        cx.op("act", lambda e, q=q, bre=bre: e.copy(out=ko[q][:, 0, :], in_=ps[:, bre, :]), r=[("ps", bre)], w=[("ko", q)])
                        cx.op("dve", lambda e, q=q, bim=bim: e.tensor_copy(out=ko[q][:, 1, :], in_=ps[:, bim, :]), r=[("ps", bim)], w=[("ko", q)])
                        cx.dma("act", lambda e, i=i, q=q, o=o, cb=cb: e.dma_start(
                            out=kf[2 * i * 128:(2 * i + 2) * 128, o, cb * 512:(cb + 1) * 512].rearrange("(c p) n -> p c n", p=128), in_=ko[q][:]),
                            ("ko", q), r=[("ko", q)])
            cx.flush()

    def phase_B(self, l):
        nc, cx, ps = self.nc, self.cx, self.ps
        inp = self.inp
        t0 = 0
        for L in self.seqs:
            NTs = L // 128
            fwdP = inp["fwd%d" % L]
            invP = inp["inv%d" % L]
            kf = self.s_kf[L]
            for cb in range(2):
                c0, c1 = cb * 512, (cb + 1) * 512
                with ExitStack() as st0:
                    zT = st0.enter_context(nc.sbuf_tensor(self.uname("b_zT"), [128, NTs, 512], BF16))
                    zv = self.s_zb.rearrange("t (a c) -> t a c", a=3)
                    cwv = inp["hy_conv_w"][l:l + 1].rearrange("o k (a c) -> o k a c", a=3)
                    cbv = inp["hy_conv_b"][l:l + 1, :].rearrange("o (a c) -> o a c", a=3)

                    def conv_chunk(j, part, tiles, cw_t, cb_t, out_ap, out_key, tag, wtag):
                        cur, prv, nxt, acc, tm = tiles
                        q = j % 2
                        r0 = t0 + j * 128
                        cx.dma("sp", lambda e: e.dma_start(out=cur[q][:], in_=zv[r0:r0 + 128, part, c0:c1]), (tag + "cur", q), w=[(tag + "cur", q)])
                        if j == 0:
                            cx.op("pool", lambda e: e.memset(prv[q][:], 0.0), w=[(tag + "prv", q)])
                            cx.dma("sp", lambda e: e.dma_start(out=prv[q][1:128], in_=zv[r0:r0 + 127, part, c0:c1]), (tag + "prv", q), w=[(tag + "prv", q)])
                        else:
                            cx.dma("sp", lambda e: e.dma_start(out=prv[q][:], in_=zv[r0 - 1:r0 + 127, part, c0:c1]), (tag + "prv", q), w=[(tag + "prv", q)])
                        if j == NTs - 1:
                            cx.op("pool", lambda e: e.memset(nxt[q][:], 0.0), w=[(tag + "nxt", q)])
                            cx.dma("sp", lambda e: e.dma_start(out=nxt[q][0:127], in_=zv[r0 + 1:r0 + 128, part, c0:c1]), (tag + "nxt", q), w=[(tag + "nxt", q)])
                        else:
                            cx.dma("sp", lambda e: e.dma_start(out=nxt[q][:], in_=zv[r0 + 1:r0 + 129, part, c0:c1]), (tag + "nxt", q), w=[(tag + "nxt", q)])
                        cx.op("dve", lambda e: e.tensor_tensor(out=acc[q][:], in0=cur[q][:], in1=cw_t[:, 1, :], op=ALU.mult),
                              r=[(tag + "cur", q), wtag + "cw"], w=[(tag + "acc", q)])
                        cx.op("pool", lambda e: e.tensor_tensor(out=tm[q][:], in0=prv[q][:], in1=cw_t[:, 0, :], op=ALU.mult),
                              r=[(tag + "prv", q), wtag + "cw"], w=[(tag + "tm", q)])
                        cx.op("dve", lambda e: e.tensor_tensor(out=acc[q][:], in0=acc[q][:], in1=tm[q][:], op=ALU.add),
                              r=[(tag + "acc", q), (tag + "tm", q)], w=[(tag + "acc", q)])
                        cx.op("pool", lambda e: e.tensor_tensor(out=tm[q][:], in0=nxt[q][:], in1=cw_t[:, 2, :], op=ALU.mult),
                              r=[(tag + "nxt", q), wtag + "cw"], w=[(tag + "tm", q)])
                        cx.op("dve", lambda e: e.tensor_tensor(out=acc[q][:], in0=acc[q][:], in1=tm[q][:], op=ALU.add),
                              r=[(tag + "acc", q), (tag + "tm", q)], w=[(tag + "acc", q)])
                        cx.op("dve", lambda e: e.tensor_tensor(out=out_ap, in0=acc[q][:], in1=cb_t, op=ALU.add),
                              r=[(tag + "acc", q), wtag + "cb"], w=[out_key])

                    with ExitStack() as st:
                        def sb(name, shape, dt):
                            return st.enter_context(nc.sbuf_tensor(self.uname(name), shape, dt))
                        cwt = sb("b_cw", [128, 3, 512], F32)
                        cbt = sb("b_cb", [128, 512], F32)
                        tiles = tuple([sb("b_%s%d" % (nm, i), [128, 512], dt) for i in range(2)]
                                      for nm, dt in (("cur", BF16), ("prv", BF16), ("nxt", BF16), ("acc", F32), ("tm", F32)))
                        cx.dma("sp", lambda e: e.dma_start(out=cwt[:], in_=cwv[:, :, 0, c0:c1].partition_broadcast(128)), "cw", w=["vcw"])
                        cx.dma("sp", lambda e: e.dma_start(out=cbt[:], in_=cbv[:, 0, c0:c1].partition_broadcast(128)), "cb", w=["vcb"])
                        for j in range(NTs):
                            conv_chunk(j, 0, tiles, cwt, cbt[:], zT[:, j, :], ("zT", j), "v", "v")
                        cx.flush()
                    with ExitStack() as st:
                        def sb(name, shape, dt):
                            return st.enter_context(nc.sbuf_tensor(self.uname(name), shape, dt))
                        Y = sb("b_Y", [128, 2 * NTs, 512], BF16)
                        SB = [sb("b_S%d" % i, [128, 2 * NTs * 128], BF16) for i in range(2)]
                        skp = sb("b_skip", [128, 2, 512], F32)
                        kft = [sb("b_kf%d" % i, [128, 2, 512], BF16) for i in range(2)]
                        tt = [sb("b_t%d" % i, [128, 512], F32) for i in range(4)]
                        xcw = [sb("b_xcw%d" % i, [128, 3, 512], F32) for i in range(2)]
                        xcb = [sb("b_xcb%d" % i, [128, 512], F32) for i in range(2)]
                        xtiles = tuple([sb("b_x%s%d" % (nm, i), [128, 512], dt) for i in range(2)]
                                       for nm, dt in (("cur", BF16), ("prv", BF16), ("nxt", BF16), ("acc", F32), ("tm", F32)))
                        xg = [sb("b_xg%d" % i, [128, 512], F32) for i in range(2)]
                        g1 = [sb("b_g1%d" % i, [128, 512], F32) for i in range(2)]
                        g2 = [sb("b_g2%d" % i, [128, 512], F32) for i in range(2)]
                        yst = [sb("b_yst%d" % i, [128, 512], BF16) for i in range(2)]
                        cx.dma("sp", lambda e: e.dma_start(out=skp[:], in_=inp["hy_skip"][l:l + 1, :, c0:c1].partition_broadcast(128)), "skip", w=["skip"])
                        for o in range(2):
                            cx.dma("sp", lambda e, o=o: e.dma_start(out=xcw[o][:], in_=cwv[:, :, 1 + o, c0:c1].partition_broadcast(128)), ("xcw", o), w=["x%dcw" % o])
                            cx.dma("sp", lambda e, o=o: e.dma_start(out=xcb[o][:], in_=cbv[:, 1 + o, c0:c1].partition_broadcast(128)), ("xcb", o), w=["x%dcb" % o])
                        it = 0
                        for o in range(2):
                            for i in range(NTs):
                                q = it % 2
                                it += 1
                                Fq = SB[q][:, 0:NTs * 256].rearrange("p (j c) -> p j c", c=256)
                                cx.dma("sp", lambda e, Fq=Fq, i=i: e.dma_start(out=Fq, in_=fwdP[i]), ("S", q), w=[("S", q)])
                                cx.dma("sp", lambda e, q=q, i=i, o=o: e.dma_start(
                                    out=kft[q][:], in_=kf[2 * i * 128:(2 * i + 2) * 128, o, c0:c1].rearrange("(c p) n -> p c n", p=128)),
                                    ("kf", q), w=[("kf", q)])
                                bre, bim = 2 * q, 2 * q + 1
                                for j in range(NTs):
                                    cx.op("pe", lambda e, Fq=Fq, j=j, bre=bre: e.matmul(ps[:, bre, :], lhsT=Fq[:, j, 0:128], rhs=zT[:, j, :],
                                                                                       start=(j == 0), stop=(j == NTs - 1)),
                                          r=[("S", q), ("zT", j)], w=[("ps", bre)])
                                    cx.op("pe", lambda e, Fq=Fq, j=j, bim=bim: e.matmul(ps[:, bim, :], lhsT=Fq[:, j, 128:256], rhs=zT[:, j, :],
                                                                                       start=(j == 0), stop=(j == NTs - 1)),
                                          r=[("S", q), ("zT", j)], w=[("ps", bim)])
                                cx.op("dve", lambda e, q=q, bre=bre: e.tensor_tensor(out=tt[0][:], in0=ps[:, bre, :], in1=kft[q][:, 0, :], op=ALU.mult),
                                      r=[("ps", bre), ("kf", q)], w=[("tt", 0)])
                                cx.op("dve", lambda e, q=q, bim=bim: e.tensor_tensor(out=tt[1][:], in0=ps[:, bim, :], in1=kft[q][:, 1, :], op=ALU.mult),
                                      r=[("ps", bim), ("kf", q)], w=[("tt", 1)])
                                cx.op("dve", lambda e, q=q, bre=bre: e.tensor_tensor(out=tt[2][:], in0=ps[:, bre, :], in1=kft[q][:, 1, :], op=ALU.mult),
                                      r=[("ps", bre), ("kf", q)], w=[("tt", 2)])
                                cx.op("dve", lambda e, q=q, bim=bim: e.tensor_tensor(out=tt[3][:], in0=ps[:, bim, :], in1=kft[q][:, 0, :], op=ALU.mult),
                                      r=[("ps", bim), ("kf", q)], w=[("tt", 3)])
                                cx.op("pool", lambda e, i=i: e.tensor_tensor(out=Y[:, 2 * i, :], in0=tt[0][:], in1=tt[1][:], op=ALU.subtract),
                                      r=[("tt", 0), ("tt", 1)], w=[("Y", 2 * i)])
                                cx.op("pool", lambda e, i=i: e.tensor_tensor(out=Y[:, 2 * i + 1, :], in0=tt[2][:], in1=tt[3][:], op=ALU.add),
                                      r=[("tt", 2), ("tt", 3)], w=[("Y", 2 * i + 1)])
                            for j in range(NTs):
                                q = it % 2
                                it += 1
                                r0 = t0 + j * 128
                                Iq = SB[q][:, :].rearrange("p (r t) -> p r t", t=128)
                                cx.dma("sp", lambda e, Iq=Iq, j=j: e.dma_start(out=Iq, in_=invP[j]), ("S", q), w=[("S", q)])
                                jq = j % 2
                                conv_chunk(j, 1 + o, xtiles, xcw[o], xcb[o][:], xg[jq][:], ("xg", jq), "x", "x%d" % o)
                                by = 4 + q
                                for r in range(2 * NTs):
                                    cx.op("pe", lambda e, Iq=Iq, r=r, by=by: e.matmul(ps[:, by, :], lhsT=Iq[:, r, :], rhs=Y[:, r, :],
                                                                                     start=(r == 0), stop=(r == 2 * NTs - 1)),
                                          r=[("S", q), ("Y", r)], w=[("ps", by)])
                                cx.op("pool", lambda e, q=q, j=j, o=o: e.tensor_tensor(out=g1[q][:], in0=zT[:, j, :], in1=skp[:, o, :], op=ALU.mult),
                                      r=[("zT", j), "skip"], w=[("g1", q)])
                                cx.op("dve", lambda e, q=q, by=by: e.tensor_tensor(out=g2[q][:], in0=ps[:, by, :], in1=g1[q][:], op=ALU.add),
                                      r=[("ps", by), ("g1", q)], w=[("g2", q)])
                                if o == 0:
                                    cx.op("dve", lambda e, q=q, j=j, jq=jq: e.tensor_tensor(out=zT[:, j, :], in0=g2[q][:], in1=xg[jq][:], op=ALU.mult),
                                          r=[("g2", q), ("xg", jq)], w=[("zT", j)])
                                else:
                                    cx.op("dve", lambda e, q=q, jq=jq: e.tensor_tensor(out=yst[q][:], in0=g2[q][:], in1=xg[jq][:], op=ALU.mult),
                                          r=[("g2", q), ("xg", jq)], w=[("yst", q)])
                                    cx.dma("act", lambda e, q=q, r0=r0: e.dma_start(out=self.s_yb[r0:r0 + 128, c0:c1], in_=yst[q][:]), ("yst", q), r=[("yst", q)])
                        cx.flush()
            t0 += L

    def phase_C(self, l):
        self.phase_C1(l)
        self.phase_C2(l)

    def phase_C1(self, l):
        nc, cx, ps = self.nc, self.cx, self.ps
        with ExitStack() as st:
            def sb(name, shape, dt):
                return st.enter_context(nc.sbuf_tensor(self.uname(name), shape, dt))
            wbo = sb("c1_wbo", [128, 8, 1024], BF16)
            woo = sb("c1_woo", [128, 8, 1024], BF16)
            xts = [sb("c1_x%d" % i, [128, D], F32) for i in range(2)]
            ybs = [sb("c1_yb%d" % i, [128, D], BF16) for i in range(2)]
            gbs = [sb("c1_gb%d" % i, [128, D], BF16) for i in range(2)]
            gys = [sb("c1_gy%d" % i, [128, D], BF16) for i in range(2)]
            ybTs = [sb("c1_ybT%d" % i, [128, 8, 128], BF16) for i in range(2)]
            t32 = sb("c1_t32", [128, D], F32)
            m = sb("c1_m", [128, D], BF16)
            mT = sb("c1_mT", [128, 8, 128], BF16)
            cx.dma("sp", lambda e: e.dma_start(out=wbo[:], in_=self.s_wbo[l].rearrange("(j p) n -> p j n", p=128)), "wbo", w=["wbo"])
            cx.dma("sp", lambda e: e.dma_start(out=woo[:], in_=self.s_wo[l].rearrange("(j p) n -> p j n", p=128)), "woo", w=["woo"])
            src = self.inp["x"] if l == 0 else self.s_xres

            def Ca(i):
                p = i % 2
                r0 = i * 128
                xt, yb, gb, gy = xts[p], ybs[p], gbs[p], gys[p]
                cx.dma("sp", lambda e: e.dma_start(out=xt[:], in_=src[r0:r0 + 128, :]), ("x", p), w=[("x", p)])
                cx.dma("sp", lambda e: e.dma_start(out=yb[:], in_=self.s_yb[r0:r0 + 128, :]), ("yb", p), w=[("yb", p)])
                cx.dma("sp", lambda e: e.dma_start(out=gb[:], in_=self.s_gb[r0:r0 + 128, :]), ("gb", p), w=[("gb", p)])
                cx.dma("sp", lambda e: e.dma_start(out=gy[:], in_=self.s_gaya[r0:r0 + 128, :]), ("gy", p), w=[("gy", p)])
                bank = 2 + p
                pv = self.psb(bank)
                for k in range(8):
                    cx.op("pe", lambda e, k=k: e.transpose(out=pv[:, k * 128:(k + 1) * 128], in_=yb[:, k * 128:(k + 1) * 128],
                                                          identity=self.identb[:]), r=[("yb", p)], w=[("ps", bank)])
                cx.op("act", lambda e: e.copy(out=ybTs[p][:].rearrange("p k t -> p (k t)"), in_=pv[:, :]), r=[("ps", bank)], w=[("ybT", p)])

            def Cb(i, pend):
                p = i % 2
                r0 = i * 128
                xt, yb, gb, gy = xts[p], ybs[p], gbs[p], gys[p]
                ybT = ybTs[p]
                for nb in range(2):
                    for k in range(8):
                        cx.op("pe", lambda e, nb=nb, k=k: e.matmul(ps[:, nb, :], lhsT=ybT[:, k, :], rhs=wbo[:, k, nb * 512:(nb + 1) * 512],
                                                                  start=(k == 0), stop=(k == 7)), r=[("ybT", p), "wbo"], w=[("ps", nb)])
                    cx.op("dve", lambda e, nb=nb: e.tensor_tensor(out=t32[:, nb * 512:(nb + 1) * 512], in0=ps[:, nb, :],
                                                                 in1=gb[:, nb * 512:(nb + 1) * 512], op=ALU.mult),
                          r=[("ps", nb), ("gb", p)], w=[("t32", nb)])
                    cx.op("pool", lambda e, nb=nb: e.tensor_tensor(out=m[:, nb * 512:(nb + 1) * 512], in0=t32[:, nb * 512:(nb + 1) * 512],
                                                                  in1=gy[:, nb * 512:(nb + 1) * 512], op=ALU.add),
                          r=[("t32", nb), ("gy", p)], w=[("m", nb)])
                cx.replay(pend, len(pend))
                pv3 = self.psb(6)
                for k in range(8):
                    cx.op("pe", lambda e, k=k: e.transpose(out=pv3[:, k * 128:(k + 1) * 128], in_=m[:, k * 128:(k + 1) * 128],
                                                          identity=self.identb[:]), r=[("m", k // 4)], w=[("ps", 6)])
                cx.op("act", lambda e: e.copy(out=mT[:].rearrange("p k t -> p (k t)"), in_=pv3[:, :]), r=[("ps", 6)], w=["mT"])
                for nb in range(2):
                    bank = 4 + nb
                    for k in range(8):
                        cx.op("pe", lambda e, nb=nb, k=k, bank=bank: e.matmul(ps[:, bank, :], lhsT=mT[:, k, :], rhs=woo[:, k, nb * 512:(nb + 1) * 512],
                                                                             start=(k == 0), stop=(k == 7)), r=["mT", "woo"], w=[("ps", bank)])
                    cx.op("dve", lambda e, nb=nb, bank=bank: e.tensor_tensor(out=xt[:, nb * 512:(nb + 1) * 512], in0=ps[:, bank, :],
                                                                            in1=xt[:, nb * 512:(nb + 1) * 512], op=ALU.add),
                          r=[("ps", bank), ("x", p)], w=[("x", p)])
                cx.dma("pool", lambda e: e.dma_start(out=self.s_xres[r0:r0 + 128, :], in_=xt[:]), ("xo", p), r=[("x", p)])

            Ca(0)
            for i in range(self.NT):
                pend = []
                if i + 1 < self.NT:
                    cx.defer_begin()
                    Ca(i + 1)
                    pend = cx.defer_end()
                Cb(i, pend)
            cx.flush()

    def phase_C2(self, l):
        nc, cx, ps = self.nc, self.cx, self.ps
        last = (l == DEPTH - 1)
        RG = 10
        with ExitStack() as st:
            def sb(name, shape, dt):
                return st.enter_context(nc.sbuf_tensor(self.uname(name), shape, dt))
            inp = self.inp
            keysT = sb("c2_keysT", [128, 16, 128], BF16)
            with ExitStack() as st2:
                kn = st2.enter_context(nc.sbuf_tensor(self.uname("c2_kn"), [128, 16, 128], F32))
                knb = st2.enter_context(nc.sbuf_tensor(self.uname("c2_knb"), [128, 16, 128], BF16))
                cx.dma("sp", lambda e: e.dma_start(out=kn[:], in_=inp["peer_keys"][l].rearrange("h t n d -> n (h t) d")), "kn", w=["kn"])
                cx.op("dve", lambda e: e.tensor_copy(out=knb[:], in_=kn[:]), r=["kn"], w=["knb"])
                for half in range(2):
                    pv = self.psb(half)
                    for j in range(8):
                        g = half * 8 + j
                        cx.op("pe", lambda e, j=j, g=g, pv=pv: e.transpose(out=pv[:, j * 128:(j + 1) * 128], in_=knb[:, g, :], identity=self.identb[:]),
                              r=["knb"], w=[("ps", half)])
                    cx.op("act", lambda e, pv=pv, half=half: e.copy(out=keysT[:, half * 8:half * 8 + 8, :].rearrange("p g n -> p (g n)"), in_=pv[:, :]),
                          r=[("ps", half)], w=["keysT"])
                cx.flush()
            wq = sb("c2_wq", [128, 8, 2048], BF16)
            gffn = sb("c2_gffn", [128, D], F32)
            gfin = sb("c2_gfin", [128, D], F32) if last else None
            iota = sb("c2_iota", [128, 16], F32)
            thr = sb("c2_thr", [128, 16], F32)
            amask = sb("c2_amask", [128, 128, 128], BF16)
            xts = [sb("c2_x%d" % i, [128, D], F32) for i in range(2)]
            tl = dict(junk=sb("c2_junk", [128, D], BF16), ss=sb("c2_ss", [128, 1], F32), rstd=sb("c2_rstd", [128, 1], F32),
                      hn=sb("c2_hn", [128, D], BF16), hT=sb("c2_hT", [128, 8, 128], BF16))
            qT = sb("c2_qT", [128, 16, 128], BF16)
            sc = sb("c2_s", [128, 16, 128], F32)
            sc2 = sb("c2_s2", [128, 16, 128], F32)
            tv = sb("c2_tv", [128, 16, 16], F32)
            ti = sb("c2_ti", [128, 16, 16], U32)
            tif = sb("c2_tif", [128, 16, 16], BF16)
            cand = sb("c2_cand", [128, 8, 256], F32)
            cand2 = sb("c2_cand2", [128, 8, 256], F32)
            bs = sb("c2_bs", [128, 8, 16], F32)
            bp = sb("c2_bp", [128, 8, 16], U32)
            pau = sb("c2_pau", [128, 8, 16], U32)
            pbu = sb("c2_pbu", [128, 8, 16], U32)
            oh = sb("c2_oh", [128, 8, 16, 16], BF16)
            paf = sb("c2_paf", [128, 8, 16], BF16)
            pbf = sb("c2_pbf", [128, 8, 16], BF16)
            iotab = sb("c2_iotab", [128, 16], BF16)
            i1 = sb("c2_i1", [128, 8, 16], F32)
            i2 = sb("c2_i2", [128, 8, 16], F32)
            idxf = sb("c2_idxf", [128, 128], F32)
            ee = sb("c2_e", [128, 8, 16], F32)
            zz = sb("c2_z", [128, 8], F32)
            gate = sb("c2_gate", [128, 8, 16], F32)
            idxTs = [sb("c2_idxT%d" % i, [128, 128], U32) for i in range(2)]
            gateTs = [sb("c2_gateT%d" % i, [128, 128], F32) for i in range(2)]
            hns = [sb("c2_hn%d" % i, [128, D], BF16) for i in range(2)]
            hTt = sb("c2_hTt", [128, 128], F32)
            gl = sb("c2_gl", [128, 128], F32)
            aT = sb("c2_aT", [128, 128], BF16)
            Gr = [sb("c2_G%d" % i, [128, 2 * D], BF16) for i in range(RG)]
            junk2 = sb("c2_junk2", [128, D], BF16)
            ost = [sb("c2_ost%d" % i, [128, D], F32) for i in range(2)]
            fin = dict(junk=tl["junk"], ss=sb("c2_fss", [128, 1], F32), rstd=sb("c2_frstd", [128, 1], F32))

            inp = self.inp
            cx.dma("sp", lambda e: e.dma_start(out=wq[:], in_=self.s_wq[l].rearrange("(j p) n -> p j n", p=128)), "wq", w=["wq"])
            cx.dma("sp", lambda e: e.dma_start(out=gffn[:], in_=inp["norm_ffn"][l:l + 1, :].partition_broadcast(128)), "g", w=["g"])
            if last:
                cx.dma("sp", lambda e: e.dma_start(out=gfin[:], in_=inp["final_norm"].rearrange("(o n) -> o n", o=1).partition_broadcast(128)),
                       "gfin", w=["gfin"])
            cx.dma("sp", lambda e: e.dma_start(out=iota[:], in_=inp["iota16"][:, :]), "iota", w=["iota"])
            cx.op("dve", lambda e: e.tensor_copy(out=iotab[:], in_=iota[:]), r=["iota"], w=["iotab"])
            cx.op("pool", lambda e: e.memset(amask[:], 0.0), w=["amask"])
            uv_tab = self.s_uv[l].rearrange("e two d -> e (two d)")
            amf = amask[:].rearrange("p t m -> p (t m)")
            def front(i):
                p = i % 2
                r0 = i * 128
                xt = xts[p]
                cx.dma("sp", lambda e, xt=xt, r0=r0: e.dma_start(out=xt[:], in_=self.s_xres[r0:r0 + 128, :]), ("x", p), w=[("x", p)])
                tlp = dict(tl); tlp["hn"] = hns[p]
                self.rms_to_hT(xt, ("x", p), gffn, tlp, bank=0, hn_out_key=("hn", p))
                xn, xnT = hns[p], tl["hT"]
                idxT, gateT = idxTs[p], gateTs[p]
                for rnd in range(2):
                    for g8 in range(8):
                        hp = rnd * 8 + g8
                        bank = g8 // 4
                        for k in range(8):
                            cx.op("pe", lambda e, hp=hp, g8=g8, bank=bank, k=k: e.matmul(
                                ps[:, bank, (g8 % 4) * 128:(g8 % 4 + 1) * 128], lhsT=wq[:, k, hp * 128:(hp + 1) * 128], rhs=xnT[:, k, :],
                                start=(k == 0), stop=(k == 7)), r=["wq", "hT"], w=[("ps", bank)])
                    cx.op("act", lambda e, rnd=rnd: e.copy(out=qT[:, rnd * 8:rnd * 8 + 8, :].rearrange("p g t -> p (g t)"),
                                                          in_=ps[:, 0:2, :].rearrange("p a n -> p (a n)")),
                          r=[("ps", 0), ("ps", 1)], w=[("qT", rnd)])
                for rnd in range(2):
                    for g8 in range(8):
                        hp = rnd * 8 + g8
                        bank = g8 // 4
                        cx.op("pe", lambda e, hp=hp, g8=g8, bank=bank: e.matmul(
                            ps[:, bank, (g8 % 4) * 128:(g8 % 4 + 1) * 128], lhsT=qT[:, hp, :], rhs=keysT[:, hp, :], start=True, stop=True),
                            r=[("qT", rnd), "keysT"], w=[("ps", bank)])
                    cx.op("act", lambda e, rnd=rnd: e.copy(out=sc[:, rnd * 8:rnd * 8 + 8, :].rearrange("p g n -> p (g n)"),
                                                          in_=ps[:, 0:2, :].rearrange("p a n -> p (a n)")),
                          r=[("ps", 0), ("ps", 1)], w=[("sc", rnd)])
                for g in range(16):
                    rk = ("sc", g // 8)
                    cx.op("dve", lambda e, g=g: e.max(out=tv[:, g, 0:8], in_=sc[:, g, :]), r=[rk], w=[("tv", g)])
                for g in range(16):
                    rk = ("sc", g // 8)
                    cx.op("dve", lambda e, g=g: e.max_index(out=ti[:, g, 0:8], in_max=tv[:, g, 0:8], in_values=sc[:, g, :]),
                          r=[rk, ("tv", g)], w=[("ti", g)])
                for g in range(16):
                    rk = ("sc", g // 8)
                    cx.op("dve", lambda e, g=g: e.match_replace(out=sc2[:, g, :], in_to_replace=tv[:, g, 0:8], in_values=sc[:, g, :], imm_value=-1e30),
                          r=[rk, ("tv", g)], w=[("sc2", g)])
                for g in range(16):
                    cx.op("dve", lambda e, g=g: e.max(out=tv[:, g, 8:16], in_=sc2[:, g, :]), r=[("sc2", g)], w=[("tv", g)])
                for g in range(16):
                    cx.op("dve", lambda e, g=g: e.max_index(out=ti[:, g, 8:16], in_max=tv[:, g, 8:16], in_values=sc2[:, g, :]),
                          r=[("sc2", g), ("tv", g)], w=[("ti", g)])
                alltv = [("tv", g) for g in range(16)]
                allti = [("ti", g) for g in range(16)]
                cx.op("dve", lambda e: e.tensor_copy(out=tif[:], in_=ti[:]), r=allti, w=["tif"])
                tv4 = tv[:].rearrange("p (h t) k -> p h t k", t=2)
                tif4 = tif[:].rearrange("p (h t) k -> p h t k", t=2)
                cand4 = cand[:].rearrange("p h (a b) -> p h a b", b=16)
                cx.op("dve", lambda e: e.tensor_tensor(out=cand4, in0=tv4[:, :, 0, :].unsqueeze(3).to_broadcast([128, 8, 16, 16]),
                                                       in1=tv4[:, :, 1, :].unsqueeze(2).to_broadcast([128, 8, 16, 16]), op=ALU.add),
                      r=alltv, w=["cand"])
                for h in range(8):
                    cx.op("dve", lambda e, h=h: e.max(out=bs[:, h, 0:8], in_=cand[:, h, :]), r=["cand"], w=[("bs", h)])
                for h in range(8):
                    cx.op("dve", lambda e, h=h: e.max_index(out=bp[:, h, 0:8], in_max=bs[:, h, 0:8], in_values=cand[:, h, :]),
                          r=["cand", ("bs", h)], w=[("bp", h)])
                for h in range(8):
                    cx.op("dve", lambda e, h=h: e.match_replace(out=cand2[:, h, :], in_to_replace=bs[:, h, 0:8], in_values=cand[:, h, :], imm_value=-1e30),
                          r=["cand", ("bs", h)], w=[("cand2", h)])
                for h in range(8):
                    cx.op("dve", lambda e, h=h: e.max(out=bs[:, h, 8:16], in_=cand2[:, h, :]), r=[("cand2", h)], w=[("bs", h)])
                for h in range(8):
                    cx.op("dve", lambda e, h=h: e.max_index(out=bp[:, h, 8:16], in_max=bs[:, h, 8:16], in_values=cand2[:, h, :]),
                          r=[("cand2", h), ("bs", h)], w=[("bp", h)])
                allbs = [("bs", h) for h in range(8)]
                allbp = [("bp", h) for h in range(8)]
                B4 = [128, 8, 16, 16]
                cx.op("dve", lambda e: e.tensor_single_scalar(out=pau[:], in_=bp[:], scalar=4, op=ALU.logical_shift_right), r=allbp, w=["pau"])
                cx.op("dve", lambda e: e.tensor_single_scalar(out=pbu[:], in_=bp[:], scalar=15, op=ALU.bitwise_and), r=allbp, w=["pbu"])
                cx.op("dve", lambda e: e.tensor_copy(out=paf[:], in_=pau[:]), r=["pau"], w=["paf"])
                cx.op("dve", lambda e: e.tensor_copy(out=pbf[:], in_=pbu[:]), r=["pbu"], w=["pbf"])
                for (pf, pk, tsel, iout, ik) in ((paf, "paf", 0, i1, "i1"), (pbf, "pbf", 1, i2, "i2")):
                    cx.op("dve", lambda e, pf=pf: e.tensor_tensor(out=oh[:], in0=pf[:].unsqueeze(3).to_broadcast(B4),
                                                                 in1=iotab[:].unsqueeze(1).unsqueeze(1).to_broadcast(B4), op=ALU.is_equal),
                          r=[pk, "iotab"], w=["oh"])
                    cx.op("dve", lambda e, tsel=tsel: e.tensor_tensor(out=oh[:], in0=oh[:], in1=tif4[:, :, tsel, :].unsqueeze(2).to_broadcast(B4),
                                                                     op=ALU.mult), r=["oh", "tif"], w=["oh"])
                    cx.op("dve", lambda e, iout=iout: e.tensor_reduce(out=iout[:], in_=oh[:], axis=AX.X, op=ALU.add), r=["oh"], w=[ik])
                cx.op("dve", lambda e: e.scalar_tensor_tensor(out=idxf[:], in0=i1[:].rearrange("p h k -> p (h k)"), scalar=128.0,
                                                              in1=i2[:].rearrange("p h k -> p (h k)"), op0=ALU.mult, op1=ALU.add),
                      r=["i1", "i2"], w=["idxf"])
                cx.op("dve", lambda e: e.tensor_tensor(out=ee[:], in0=bs[:], in1=bs[:, :, 0:1].to_broadcast([128, 8, 16]), op=ALU.subtract),
                      r=allbs, w=["ee"])
                cx.op("act", lambda e: e.activation(out=ee[:], in_=ee[:], func=AF.Exp), r=["ee"], w=["ee"])
                cx.op("dve", lambda e: e.tensor_reduce(out=zz[:], in_=ee[:], axis=AX.X, op=ALU.add), r=["ee"], w=["zz"])
                cx.op("dve", lambda e: e.reciprocal(out=zz[:], in_=zz[:]), r=["zz"], w=["zz"])
                cx.op("dve", lambda e: e.tensor_tensor(out=gate[:], in0=ee[:], in1=zz[:].unsqueeze(2).to_broadcast([128, 8, 16]), op=ALU.mult),
                      r=["ee", "zz"], w=["gate"])
                cx.op("pe", lambda e: e.transpose(out=ps[:, 0, 0:128], in_=idxf[:], identity=self.identf[:]), r=["idxf"], w=[("ps", 0)])
                cx.op("dve", lambda e: e.tensor_copy(out=idxT[:], in_=ps[:, 0, 0:128]), r=[("ps", 0)], w=[("idxT", p)])
                cx.op("pe", lambda e: e.transpose(out=ps[:, 1, 0:128], in_=gate[:].rearrange("p h k -> p (h k)"), identity=self.identf[:]),
                      r=["gate"], w=[("ps", 1)])
                cx.op("act", lambda e: e.copy(out=gateT[:], in_=ps[:, 1, 0:128]), r=[("ps", 1)], w=[("gateT", p)])
                LAG = 2

            def tokloop(i, pending):
                p = i % 2
                r0 = i * 128
                xt = xts[p]
                xn = hns[p]
                idxT, gateT = idxTs[p], gateTs[p]
                nrep = (len(pending) + 111) // 112 if pending else 0
                def gather(t):
                    sg = t % RG
                    cx.dma("pool", lambda e, sg=sg, t=t: e.indirect_dma_start(
                        out=Gr[sg][:], out_offset=None, in_=uv_tab, in_offset=bass.IndirectOffsetOnAxis(ap=idxT[:, t:t + 1], axis=0)),
                        ("G", sg), r=[("idxT", p)], w=[("G", sg)])

                def udot(t):
                    sg = t % RG
                    bb = 4 + 2 * (t % 2)
                    for nb in range(2):
                        cx.op("pe", lambda e, t=t, nb=nb, bb=bb: e.matmul(ps[:, bb + nb, :], lhsT=self.identb[:, t:t + 1].to_broadcast([128, 128]),
                                                                         rhs=xn[:, nb * 512:(nb + 1) * 512], start=True, stop=True),
                              r=[("hn", p), "identb"], w=[("ps", bb + nb)])
                    cx.op("dve", lambda e, t=t, sg=sg, bb=bb: e.scalar_tensor_tensor(
                        out=junk2[:], in0=Gr[sg][:, 0:D], scalar=1.0, in1=ps[:, bb:bb + 2, :].rearrange("p a n -> p (a n)"),
                        op0=ALU.mult, op1=ALU.mult, accum_out=hTt[:, t:t + 1]),
                        r=[("G", sg), ("ps", bb), ("ps", bb + 1)], w=["junk2", ("hTt", t)])
                    cx.op("act", lambda e, t=t: e.activation(out=gl[:, t:t + 1], in_=hTt[:, t:t + 1], func=AF.Gelu), r=[("hTt", t)], w=[("gl", t)])

                def acol(t):
                    cx.op("pool", lambda e, t=t: e.tensor_tensor(out=amask[:, t, t:t + 1], in0=gl[:, t:t + 1], in1=gateT[:, t:t + 1], op=ALU.mult),
                          r=[("gl", t), ("gateT", p)], w=[("am", t)])

                def vmm(t):
                    sg = t % RG
                    for nb in range(2):
                        cx.op("pe", lambda e, t=t, nb=nb, sg=sg: e.matmul(ps[:, 2 + nb, :], lhsT=amask[:, t, :],
                                                                         rhs=Gr[sg][:, D + nb * 512:D + (nb + 1) * 512],
                                                                         start=(t == 0), stop=(t == 127)),
                              r=[("am", t), "amask", ("G", sg)], w=[("ps", 2 + nb)])

                LAG = 4
                for t in range(RG):
                    gather(t)
                for t in range(128 + LAG):
                    if t < 128:
                        udot(t)
                    if 1 <= t <= 128:
                        acol(t - 1)
                    if t >= LAG:
                        vmm(t - LAG)
                        if t - LAG + RG < 128:
                            gather(t - LAG + RG)
                    if pending and 8 <= t < 120:
                        cx.replay(pending, nrep)
                ot = ost[p]
                for nb in range(2):
                    cx.op("dve", lambda e, nb=nb, ot=ot, xt=xt: e.tensor_tensor(out=ot[:, nb * 512:(nb + 1) * 512], in0=ps[:, 2 + nb, :],
                                                                               in1=xt[:, nb * 512:(nb + 1) * 512], op=ALU.add),
                          r=[("ps", 2 + nb), ("x", p)], w=[("ost", p)])
                if not last:
                    cx.dma("act", lambda e, ot=ot, r0=r0: e.dma_start(out=self.s_xres[r0:r0 + 128, :], in_=ot[:]), ("ost", p), r=[("ost", p)])
                else:
                    cx.op("act", lambda e, ot=ot: e.activation(out=fin["junk"][:], in_=ot[:], func=AF.Square, accum_out=fin["ss"][:, 0:1]),
                          r=[("ost", p)], w=["junk", "fss"])
                    cx.op("dve", lambda e: e.tensor_scalar(out=fin["rstd"][:], in0=fin["ss"][:], scalar1=1.0 / D, scalar2=RMS_EPS,
                                                           op0=ALU.mult, op1=ALU.add), r=["fss"], w=["frstd"])
                    cx.op("act", lambda e: e.sqrt(out=fin["rstd"][:], in_=fin["rstd"][:]), r=["frstd"], w=["frstd"])
                    cx.op("dve", lambda e: e.reciprocal(out=fin["rstd"][:], in_=fin["rstd"][:]), r=["frstd"], w=["frstd"])
                    cx.op("dve", lambda e, ot=ot: e.scalar_tensor_tensor(out=ot[:], in0=ot[:], scalar=fin["rstd"][:, 0:1], in1=gfin[:],
                                                                        op0=ALU.mult, op1=ALU.mult), r=[("ost", p), "frstd", "gfin"], w=[("ost", p)])
                    cx.dma("act", lambda e, ot=ot, r0=r0: e.dma_start(out=self.y[r0:r0 + 128, :], in_=ot[:]), ("ost", p), r=[("ost", p)])
            cx.defer_begin()
            front(0)
            pend = cx.defer_end()
            cx.replay(pend, len(pend))
            for i in range(self.NT):
                pend = []
                if i + 1 < self.NT:
                    cx.defer_begin()
                    front(i + 1)
                    pend = cx.defer_end()
                tokloop(i, pend)
                cx.replay(pend, len(pend))
            cx.flush()


def _flat128(ap):
    nd = len(ap.shape)
    names = " ".join("d%d" % i for i in range(nd))
    flat = ap.rearrange("%s -> (%s)" % (names, names))
    return flat.rearrange("(p n) -> p n", p=128)


FULL_SEQS = (2048, 4096, 4096)
_CACHE = {}


def const_inputs(Ls):
    c = {
        "ident_b": np.eye(128, dtype=np.float32).astype(ml_dtypes.bfloat16),
        "ident_f": np.eye(128, dtype=np.float32),
        "absdelta": abs_deltas(),
        "iota16": np.tile(np.arange(16, dtype=np.float32)[None, :], (128, 1)),
    }
    for L in Ls:
        fwd, inv = dft_mats(L)
        featT, negt = filter_feats(L)
        c["fwd%d" % L] = fwd
        c["inv%d" % L] = inv
        c["featT%d" % L] = featT
        c["negt%d" % L] = negt
    return c


def kernel(**inputs):
    xp = np.asarray(inputs["x_prompt"], dtype=np.float32)
    xs = np.asarray(inputs["x_sample"], dtype=np.float32)
    n = 8
    prog = Prog(FULL_SEQS)
    nc = prog.build()
    consts = const_inputs(prog.Ls)
    weights = {nm: np.ascontiguousarray(np.asarray(inputs[nm], dtype=np.float32)) for nm, _ in WEIGHT_SPECS}
    in_maps = []
    for c in range(n):
        xcat = np.concatenate([xp[c], xs[2 * c], xs[2 * c + 1]], axis=0)
        m = {"x": np.ascontiguousarray(xcat)}
        m.update(weights)
        m.update(consts)
        in_maps.append(m)
    res = run_bass_kernel_spmd(nc, in_maps, core_ids=list(range(n)))
    yp = np.empty_like(xp)
    ys = np.empty_like(xs)
    for c in range(n):
        y = res.results[c]["y"]
        yp[c] = y[0:2048]
        ys[2 * c] = y[2048:2048 + 4096]
        ys[2 * c + 1] = y[2048 + 4096:]
    return (yp, ys)
```
